# Optimizing a Trainium2 kernel written in Bass

```python
import math, functools
import jax, jax.numpy as jnp
from jax import lax
import numpy as np

D_MODEL = 4096
BATCH = 4
SEQ = 4096
DEPTH = 1

CHUNK = 64
LEFT_CHUNKS = 8
BAND = LEFT_CHUNKS + 1
ATT_WIDTH = D_MODEL // 2
CONV_WIDTH = D_MODEL - ATT_WIDTH
N_HEADS = 16
HEAD_DIM = ATT_WIDTH // N_HEADS
MAX_REL = 256
CONV_K = 31
PLE_DIM = 256
EPS = 1e-6
NEG_INF = -1e30
IN_COLS = 4 * ATT_WIDTH + 3 * CONV_WIDTH

kernel_name = "hymba_chunked_attn_conformer_conv_block"


def rms_norm(x, g):
    xf = x.astype(jnp.float32)
    y = xf * lax.rsqrt(jnp.mean(xf * xf, axis=-1, keepdims=True) + EPS)
    return (y * g.astype(jnp.float32)).astype(x.dtype)


def layer_norm(x, g, b):
    xf = x.astype(jnp.float32)
    mu = jnp.mean(xf, axis=-1, keepdims=True)
    xc = xf - mu
    var = jnp.mean(xc * xc, axis=-1, keepdims=True)
    y = xc * lax.rsqrt(var + EPS) * g.astype(jnp.float32) + b.astype(jnp.float32)
    return y.astype(x.dtype)


def rel_bias(table):
    i = np.arange(CHUNK)[:, None, None]
    j = np.arange(BAND)[None, :, None]
    k = np.arange(CHUNK)[None, None, :]
    dist = (LEFT_CHUNKS - j) * CHUNK + i - k
    idx = np.clip(dist, -MAX_REL, MAX_REL) + MAX_REL
    return table[:, idx].astype(jnp.float32)


def chunked_attention(q, k, v, table):
    B, T, H, Dh = q.shape
    NC = T // CHUNK
    qc = (q * (Dh ** -0.5)).reshape(B, NC, CHUNK, H, Dh)
    pad = ((0, 0), (LEFT_CHUNKS * CHUNK, 0), (0, 0), (0, 0))
    kc = jnp.pad(k, pad).reshape(B, NC + LEFT_CHUNKS, CHUNK, H, Dh)
    vc = jnp.pad(v, pad).reshape(B, NC + LEFT_CHUNKS, CHUNK, H, Dh)
    s = jnp.stack(
        [jnp.einsum('bcqhd,bckhd->bhcqk', qc, kc[:, j:j + NC],
                    preferred_element_type=jnp.float32) for j in range(BAND)],
        axis=4)
    s = s + rel_bias(table)[None, :, None]
    valid = (np.arange(NC)[:, None] - LEFT_CHUNKS + np.arange(BAND)[None, :]) >= 0
    s = jnp.where(valid[None, None, :, None, :, None], s, NEG_INF)
    pr = jax.nn.softmax(s.reshape(B, H, NC, CHUNK, BAND * CHUNK), axis=-1)
    pr = pr.reshape(B, H, NC, CHUNK, BAND, CHUNK).astype(v.dtype)
    o = functools.reduce(
        lambda acc, t: acc + t,
        [jnp.einsum('bhcqk,bckhd->bcqhd', pr[:, :, :, :, j], vc[:, j:j + NC]) for j in range(BAND)])
    return o.reshape(B, T, H * Dh)


def conv_module(a, g, w_dw, b_dw, ln_g, ln_b, w_pw, b_pw):
    u = a * jax.nn.sigmoid(g)
    u = jnp.pad(u, ((0, 0), (CONV_K - 1, 0), (0, 0)))
    u = lax.conv_general_dilated(
        u, w_dw[:, None, :], window_strides=(1,), padding='VALID',
        dimension_numbers=('NWC', 'WIO', 'NWC'),
        feature_group_count=CONV_WIDTH) + b_dw
    u = jax.nn.silu(layer_norm(u, ln_g, ln_b))
    return u @ w_pw + b_pw


def setup_inputs(seed: int = 0) -> dict:
    key = jax.random.key(seed)
    ks = jax.random.split(key, 20)
    n = jax.random.normal
    f = jnp.float32
    return {
        "x": n(ks[0], (BATCH, SEQ, D_MODEL), f),
        "p": n(ks[1], (DEPTH, BATCH, SEQ, PLE_DIM), f),
        "norm_in_g": 1.0 + 0.05 * n(ks[2], (DEPTH, D_MODEL), f),
        "w_in": n(ks[3], (DEPTH, D_MODEL, IN_COLS), f) * D_MODEL ** -0.5,
        "rel_table": 0.1 * n(ks[4], (DEPTH, N_HEADS, 2 * MAX_REL + 1), f),
        "w_dw": n(ks[5], (DEPTH, CONV_K, CONV_WIDTH), f) * CONV_K ** -0.5,
        "b_dw": 0.02 * n(ks[6], (DEPTH, CONV_WIDTH), f),
        "conv_ln_g": 1.0 + 0.05 * n(ks[7], (DEPTH, CONV_WIDTH), f),
        "conv_ln_b": 0.02 * n(ks[8], (DEPTH, CONV_WIDTH), f),
        "w_pw": n(ks[9], (DEPTH, CONV_WIDTH, CONV_WIDTH), f) * CONV_WIDTH ** -0.5,
        "b_pw": 0.02 * n(ks[10], (DEPTH, CONV_WIDTH), f),
        "attn_out_g": 1.0 + 0.05 * n(ks[11], (DEPTH, ATT_WIDTH), f),
        "conv_out_g": 1.0 + 0.05 * n(ks[12], (DEPTH, CONV_WIDTH), f),
        "w_out": n(ks[13], (DEPTH, D_MODEL, D_MODEL), f) * D_MODEL ** -0.5,
        "ple_norm_g": 1.0 + 0.05 * n(ks[14], (DEPTH, D_MODEL), f),
        "w_ple_gate": n(ks[15], (DEPTH, D_MODEL, D_MODEL), f) * D_MODEL ** -0.5,
        "b_ple_gate": 0.02 * n(ks[16], (DEPTH, D_MODEL), f),
        "w_ple": n(ks[17], (DEPTH, PLE_DIM, D_MODEL), f) * PLE_DIM ** -0.5,
        "final_g": 1.0 + 0.05 * n(ks[18], (D_MODEL,), f),
    }


def reference(x, p, norm_in_g, w_in, rel_table, w_dw, b_dw, conv_ln_g, conv_ln_b,
              w_pw, b_pw, attn_out_g, conv_out_g, w_out, ple_norm_g, w_ple_gate,
              b_ple_gate, w_ple, final_g):
    B, T, _ = x.shape
    h = x
    for i in range(DEPTH):
        xn = rms_norm(h, norm_in_g[i])
        proj = xn @ w_in[i]
        q, k, v, z_a, a_c, g_c, z_c = jnp.split(
            proj, np.cumsum([ATT_WIDTH] * 4 + [CONV_WIDTH] * 2)[:6].tolist(), axis=-1)
        shp = (B, T, N_HEADS, HEAD_DIM)
        y_a = chunked_attention(q.reshape(shp), k.reshape(shp), v.reshape(shp), rel_table[i])
        y_a = rms_norm(y_a, attn_out_g[i]) * jax.nn.silu(z_a)
        y_c = conv_module(a_c, g_c, w_dw[i], b_dw[i], conv_ln_g[i], conv_ln_b[i], w_pw[i], b_pw[i])
        y_c = rms_norm(y_c, conv_out_g[i]) * jax.nn.silu(z_c)
        h = h + jnp.concatenate([y_a, y_c], axis=-1) @ w_out[i]
        gate = jax.nn.sigmoid(rms_norm(h, ple_norm_g[i]) @ w_ple_gate[i] + b_ple_gate[i])
        h = h + gate * (p[i] @ w_ple[i])
    return rms_norm(h, final_g)
```

```python
import numpy as np
import concourse.bass as bass
import concourse.mybir as mybir
from concourse.bass_utils import run_bass_kernel_spmd

F32 = mybir.dt.float32
BF16 = mybir.dt.bfloat16
AF = mybir.ActivationFunctionType
ALU = mybir.AluOpType
AX = mybir.AxisListType

D = 4096
AW = 2048
NH = 16
TT = 512
EPS = 1e-6
NEG = -1e30
NSLOT = 6
NKS = 4
NCV = 160 + 16 * 31 + 1
C_HM = 160 + 16 * 31
C_NING, C_PNG, C_BDW, C_LNG, C_LNB, C_BPW, C_AOG, C_COG, C_WDW = 0, 32, 64, 80, 96, 112, 128, 144, 160
ENG = ("PE", "ACT", "DVE", "POOL", "SP")
import os
OPT_XH = os.environ.get('KOPT_XH', '1') == '1'
OPT_PIPE = os.environ.get('KOPT_PIPE', '1') == '1'
OPT_P0ACT = os.environ.get('KOPT_P0ACT', '0') == '1'
OPT_P5 = os.environ.get('KOPT_P5', '1') == '1'


class Res:
    __slots__ = ("name", "w", "r")

    def __init__(self, name):
        self.name = name
        self.w = None
        self.r = {}


class Sched:
    def __init__(self):
        self.prog = {e: [] for e in ENG}
        self.cnt = {e: 0 for e in ENG}
        self.waited = {e: {} for e in ENG}
        self.dcnt = {}
        self.nins = 0

    def _wait(self, eng, dep):
        key, val, tag = dep
        if tag == eng and eng == "PE":
            return
        if self.waited[eng].get(key, 0) >= val:
            return
        self.prog[eng].append(("wait", key, val))
        self.waited[eng][key] = val

    def _deps(self, eng, reads, writes):
        for r in reads:
            if r.w is not None:
                self._wait(eng, r.w)
        for w in writes:
            if w.w is not None:
                self._wait(eng, w.w)
            for k, (v, t) in w.r.items():
                self._wait(eng, (k, v, t))

    def op(self, eng, fn, reads=(), writes=(), inc=True):
        self._deps(eng, reads, writes)
        if inc:
            self.cnt[eng] += 1
            val = self.cnt[eng]
            self.prog[eng].append(("op", fn, eng, 1))
        else:
            val = self.cnt[eng] + 1
            self.prog[eng].append(("op", fn, None, 0))
        self.nins += 1
        for r in reads:
            r.r[eng] = (val, eng)
        for w in writes:
            w.w = (eng, val, eng)
            w.r = {}

    def dma(self, q, fn, sem, reads=(), writes=()):
        prev = self.dcnt.get(sem, 0)
        if prev:
            self._wait(q, (sem, prev, "dma"))
        self._deps(q, reads, writes)
        val = prev + 16
        self.dcnt[sem] = val
        self.prog[q].append(("op", fn, sem, 16))
        self.nins += 1
        for r in reads:
            r.r[sem] = (val, "dma")
        for w in writes:
            w.w = (sem, val, "dma")
            w.r = {}

    def final_wait_all(self, eng):
        for sem, val in self.dcnt.items():
            self._wait(eng, (sem, val, "dma"))


class Pool:
    def __init__(self, bufs):
        self.bufs = bufs
        self.i = 0

    def get(self):
        b = self.bufs[self.i % len(self.bufs)]
        self.i += 1
        return b


def build(NPASS=4, HALO=True):
    nc = bass.Bass("TRN2", target_bir_lowering=False)
    NTX = (TT if HALO else 0) + TT * NPASS
    NTY = TT * NPASS
    dr = lambda n, s: nc.dram_tensor(n, s, F32, kind="ExternalInput").ap()
    xc = dr("xc", [NTX, D])
    pc = dr("pc", [NTY, 256])
    w_in = dr("w_in", [D, 14336])
    w_pw = dr("w_pw", [AW, AW])
    w_out = dr("w_out", [D, D])
    w_pg = dr("w_pg", [D, D])
    w_ple = dr("w_ple", [256, D])
    cvec_d = dr("cvec", [128, NCV])
    bpg_d = dr("bpg", [1, D])
    fg_d = dr("fg", [1, D])
    bias_d = dr("bias", [NH, 128, 640])
    ident_d = dr("ident", [128, 128])
    wdg = dr("wdg", [16 * 128, 4096])
    y = nc.dram_tensor("y", [NTY, D], F32, kind="ExternalOutput").ap()

    S = Sched()
    sb = lambda n, s, dt=F32: nc.alloc_sbuf_tensor(n, s, dt)
    kts = sb("kts", [128, 20, 512], BF16)
    vts = sb("vts", [128, 20, 512], BF16)
    wb = sb("wb", [128, NSLOT, NKS * 512], BF16)
    yT = sb("yT", [128, 32, 512], BF16)
    CF = sb("CF", [128, 16384], F32)
    cvec = sb("cvec_sb", [128, NCV], F32)
    ident = sb("ident_sb", [128, 128], F32)
    identb = sb("identb", [128, 128], BF16)
    onesm = sb("onesm", [128, 128], F32)
    onesc = sb("onesc", [128, 1], F32)
    uhist = sb("uhist", [128, 16, 30], F32)
    f4 = sb("f4", [128, 4, 544], F32)
    xh = sb("xh", [128, 32, 32], BF16)
    ub = sb("ub", [128, 4, 544], BF16)
    qT = sb("qT", [128, 4, 512], BF16)
    small = sb("small", [128, 64], F32)
    NF = 4
    lbufs = [sb(f"lb{i}", [128, 640], F32) for i in range(2)]
    fbufs = [sb(f"fb{i}", [128, 640], F32) for i in range(NF)]
    NB = 4
    bbufs = [sb(f"bb{i}", [128, 640], BF16) for i in range(NB)]
    banks = [nc.alloc_psum_tensor(f"bank{i}", [128, 512], F32) for i in range(8)]

    CFb = CF[:, 0:8192].bitcast(BF16)
    xnT = CFb.rearrange("p (k t) -> p k t", t=512)
    cv = CF[:, 8192:16384].rearrange("p (c t) -> p c t", t=512)
    hview = CF[:, :].rearrange("p (a f) -> p a f", f=4096)
    xt_bufs = [CF[:, 8192:12288], CF[:, 12288:16384]]

    R_B = [Res(f"B{i}") for i in range(64)]
    R_xn = lambda kc: [R_B[kc]]
    R_cv = lambda cc: [R_B[32 + 2 * cc], R_B[33 + 2 * cc]]
    R_h = lambda tt, cb: [R_B[tt * 16 + cb * 2], R_B[tt * 16 + cb * 2 + 1]]
    R_hrow = lambda tt: R_B[tt * 16:(tt + 1) * 16]
    R_xt = lambda i: R_B[32 + 16 * i:48 + 16 * i]
    R_kt = [Res(f"kt{i}") for i in range(20)]
    R_vt = [Res(f"vt{i}") for i in range(20)]
    R_wb = [Res(f"wb{i}") for i in range(NSLOT)]
    R_yT = [Res(f"yT{i}") for i in range(32)]
    R_bank = [Res(f"bank{i}") for i in range(8)]
    R_cvec, R_ident, R_identb, R_ones = Res("cvec"), Res("ident"), Res("identb"), Res("ones")
    R_uhist = [Res(f"uh{i}") for i in range(16)]
    R_f4 = [Res(f"f4{i}") for i in range(4)]
    R_xh = Res("xh")
    R_ub = [Res(f"ub{i}") for i in range(4)]
    junk, R_junk = f4[:, 0, 0:512], R_f4[0]
    R_qT = [Res(f"qT{i}") for i in range(4)]
    R_small = {}
    FP = Pool([(fbufs[i], Res(f"fb{i}")) for i in range(NF)])
    BP = Pool([(bbufs[i], Res(f"bb{i}")) for i in range(NB)])
    LP = Pool([(lbufs[i], Res(f"lb{i}")) for i in range(2)])

    def sm(name, idx, n=1):
        if name not in R_small:
            R_small[name] = Res(name)
        return small[:, idx:idx + n], R_small[name]

    cvc = lambda c: cvec[:, c:c + 1]

    def ACT(out, in_, func, reads, writes, bias=None, scale=None, accum=None):
        kw = {}
        if bias is not None:
            kw["bias"] = bias
        if scale is not None:
            kw["scale"] = scale
        if accum is not None:
            kw["accum_out"] = accum
        S.op("ACT", lambda e: e.activation(out=out, in_=in_, func=func, **kw), reads, writes)

    def TS(out, in0, s1, op0, reads, writes, s2=None, op1=None, eng="DVE"):
        if op1 is None:
            S.op(eng, lambda e: e.tensor_scalar(out=out, in0=in0, scalar1=s1, scalar2=None, op0=op0), reads, writes)
        else:
            S.op(eng, lambda e: e.tensor_scalar(out=out, in0=in0, scalar1=s1, scalar2=s2, op0=op0, op1=op1), reads, writes)

    def STT(out, in0, sc, in1, op0, op1, reads, writes):
        S.op("DVE", lambda e: e.scalar_tensor_tensor(out=out, in0=in0, scalar=sc, in1=in1, op0=op0, op1=op1), reads, writes)

    def TTo(out, in0, in1, op, reads, writes, eng="DVE"):
        S.op(eng, lambda e: e.tensor_tensor(out=out, in0=in0, in1=in1, op=op), reads, writes)

    def CP(eng, out, in_, reads, writes):
        if eng == "ACT":
            S.op("ACT", lambda e: e.copy(out=out, in_=in_), reads, writes)
        else:
            S.op(eng, lambda e: e.tensor_copy(out=out, in_=in_), reads, writes)

    def MM(out, lhsT, rhs, start, stop, reads, writes, inc=True):
        S.op("PE", lambda e: e.matmul(out, lhsT, rhs, start=start, stop=stop), reads, writes, inc=inc)

    def TR(out, in_, idn, reads, writes, inc=True):
        S.op("PE", lambda e: e.transpose(out, in_, idn), reads, writes, inc=inc)

    def DMA(q, out, in_, sem, reads, writes):
        S.dma(q, lambda e: e.dma_start(out=out, in_=in_), sem, reads, writes)

    def rstd_from(ssq_ap, out_ap, n, reads, writes, inv):
        TS(out_ap, ssq_ap, inv, ALU.mult, reads, writes, s2=EPS, op1=ALU.add)
        ACT(out_ap, out_ap, AF.Sqrt, writes, writes)
        S.op("DVE", lambda e: e.reciprocal(out=out_ap, in_=out_ap), writes, writes)

    wstate = {"i": 0}

    def wload(src, nk=NKS):
        i = wstate["i"] % NSLOT
        wstate["i"] += 1
        dst = wb[:, i, 0:nk * 512].rearrange("p (k c) -> p k c", c=512)
        DMA("POOL", dst, src.rearrange("(k p) c -> p k c", p=128), f"w{i}", [], [R_wb[i]])
        return dst, R_wb[i]

    def wload_flat(src):
        i = wstate["i"] % NSLOT
        wstate["i"] += 1
        dst = wb[:, i, 0:2048]
        DMA("POOL", dst, src, f"w{i}", [], [R_wb[i]])
        return dst, R_wb[i]

    def proj_group(wmat, c0, nkg, rhs_fn, rhs_res_fn, bset=(0, 1, 2, 3), extra_bank=None):
        for g in range(nkg):
            slot, rs = wload(wmat[g * 512:(g + 1) * 512, c0:c0 + 512])
            for kc in range(NKS):
                k = g * NKS + kc
                for j in range(4):
                    last = (g == nkg - 1 and kc == NKS - 1)
                    if extra_bank is not None:
                        MM(banks[extra_bank][:, j * 32:(j + 1) * 32], slot[:, kc, j * 128:(j + 1) * 128], xh[:, k, :],
                           start=(k == 0 and j == 0), stop=(last and j == 3), reads=[rs, R_xh],
                           writes=[R_bank[extra_bank]], inc=False)
                    MM(banks[bset[j]][:, :], slot[:, kc, j * 128:(j + 1) * 128], rhs_fn(k),
                       start=(k == 0), stop=last, reads=[rs] + rhs_res_fn(k), writes=[R_bank[bset[j]]],
                       inc=(last or (kc == NKS - 1 and j == 3)))

    xn_rhs = lambda k: xnT[:, k, :]

    DMA("SP", cvec[:, :], cvec_d[:, :], "c0", [], [R_cvec])
    DMA("SP", ident[:, :], ident_d[:, :], "c1", [], [R_ident])
    S.op("DVE", lambda e: e.memset(onesm[:, :], 1.0 / 2048.0), [], [R_ones])
    S.op("DVE", lambda e: e.memset(onesc[:, :], 1.0), [], [R_ones])
    S.op("DVE", lambda e: e.tensor_copy(out=identb[:, :], in_=ident[:, :]), [R_ident], [R_identb])
    S.op("DVE", lambda e: e.memset(uhist[:, :, :], 0.0), [], R_uhist)
    for i in range(20):
        S.op("POOL", lambda e, i=i: e.memset(kts[:, i, :], 0.0), [], [R_kt[i]])
        S.op("POOL", lambda e, i=i: e.memset(vts[:, i, :], 0.0), [], [R_vt[i]])

    hist = list(range(16))
    free = [16, 17, 18, 19]
    xtc = {"i": 0}

    def rms_rows(src_fn, src_res, dst_small, inv):
        ss8 = small[:, 0:8]
        r8 = [sm(f"ss8_{c}", c, 1)[1] for c in range(8)]
        for c in range(8):
            ACT(f4[:, c % 4, 0:512], src_fn(c), AF.Square, src_res, [R_f4[c % 4], r8[c]], accum=ss8[:, c:c + 1])
        tot, rt = sm("sstot", 8, 1)
        S.op("DVE", lambda e: e.tensor_reduce(out=tot, in_=ss8, axis=AX.X, op=ALU.add), r8, [rt])
        rstd_from(tot, dst_small[0], 1, [rt], [dst_small[1]], inv)

    def phase0(t0):
        for tt in range(4):
            bi = xtc["i"] % 2
            xtc["i"] += 1
            xt = xt_bufs[bi]
            rx = R_xt(bi)
            DMA("SP", xt, xc[t0 + tt * 128:t0 + (tt + 1) * 128, :], f"x{bi}", [], rx)
            rs = sm("rstdx", 9, 1)
            rms_rows(lambda c: xt[:, c * 512:(c + 1) * 512], rx, rs, 1.0 / D)
            TS(xt, xt, rs[0], ALU.mult, rx + [rs[1]], rx)
            for k4 in range(8):
                bk = 4 + (k4 % 2)
                for j in range(4):
                    kc = k4 * 4 + j
                    TR(banks[bk][:, j * 128:(j + 1) * 128], xt[:, kc * 128:(kc + 1) * 128], ident[:, :],
                       rx + [R_ident], [R_bank[bk]], inc=(j == 3))
                for j in range(4):
                    kc = k4 * 4 + j
                    if j % 2 == 0 or not OPT_P0ACT:
                        TS(xnT[:, kc, tt * 128:(tt + 1) * 128], banks[bk][:, j * 128:(j + 1) * 128], cvc(C_NING + kc),
                           ALU.mult, [R_bank[bk], R_cvec], R_xn(kc))
                    else:
                        ACT(xnT[:, kc, tt * 128:(tt + 1) * 128], banks[bk][:, j * 128:(j + 1) * 128], AF.Copy,
                            [R_bank[bk], R_cvec], R_xn(kc), scale=cvc(C_NING + kc))

    def conv_phase(first):
        for cgp in range(4):
            proj_group(w_in, 10240 + cgp * 512, 8, xn_rhs, R_xn, extra_bank=(6 if first else None))
            for j in range(4):
                ACT(f4[:, j, 0:512], banks[j][:, :], AF.Sigmoid, [R_bank[j]], [R_f4[j]])
                if first:
                    ACT(f4[:, j, 512:542], banks[6][:, j * 32 + 2:j * 32 + 32], AF.Sigmoid, [R_bank[6]], [R_f4[j]])
            proj_group(w_in, 8192 + cgp * 512, 8, xn_rhs, R_xn, extra_bank=(7 if first else None))
            for j in range(4):
                cc = cgp * 4 + j
                TTo(ub[:, j, 30:542], banks[j][:, :], f4[:, j, 0:512], ALU.mult, [R_bank[j], R_f4[j]], [R_ub[j]])
                if first:
                    TTo(ub[:, j, 0:30], banks[7][:, j * 32 + 2:j * 32 + 32], f4[:, j, 512:542], ALU.mult,
                        [R_bank[7], R_f4[j]], [R_ub[j]])
                else:
                    CP("ACT", ub[:, j, 0:30], uhist[:, cc, :], [R_uhist[cc]], [R_ub[j]])
                CP("ACT", uhist[:, cc, :], ub[:, j, 512:542], [R_ub[j]], [R_uhist[cc]])
            for j in range(4):
                cc = cgp * 4 + j
                rc = R_cv(cc)
                for half in range(2):
                    wsl, rws = wload_flat(wdg[cc * 128:(cc + 1) * 128, half * 2048:(half + 1) * 2048])
                    ntap = 16 if half == 0 else 15
                    for t in range(ntap):
                        tap = half * 16 + t
                        MM(banks[j][:, :], wsl[:, t * 128:(t + 1) * 128], ub[:, j, tap:tap + 512],
                           start=(tap == 0), stop=(tap == 30), reads=[rws, R_ub[j]], writes=[R_bank[j]],
                           inc=(t == ntap - 1))
                ACT(cv[:, cc, :], banks[j][:, :], AF.Identity, [R_bank[j], R_cvec], rc, bias=cvc(C_BDW + cc))
                sq, rsq = FP.get()
                ACT(sq[:, 0:512], cv[:, cc, :], AF.Square, rc, [rsq])
                MM(banks[4][:, :], onesm[:, :], cv[:, cc, :], start=(cc == 0), stop=(cc == 15),
                   reads=[R_ones] + rc, writes=[R_bank[4]])
                MM(banks[5][:, :], onesm[:, :], sq[:, 0:512], start=(cc == 0), stop=(cc == 15),
                   reads=[R_ones, rsq], writes=[R_bank[5]])
        mean, rm = LP.get()
        rstd, rr = LP.get()
        CP("DVE", mean[:, 0:512], banks[4][:, :], [R_bank[4]], [rm])
        TTo(rstd[:, 0:512], mean[:, 0:512], mean[:, 0:512], ALU.mult, [rm], [rr])
        TTo(rstd[:, 0:512], banks[5][:, :], rstd[:, 0:512], ALU.subtract, [R_bank[5], rr], [rr])
        TS(rstd[:, 0:512], rstd[:, 0:512], EPS, ALU.add, [rr], [rr])
        ACT(rstd[:, 0:512], rstd[:, 0:512], AF.Sqrt, [rr], [rr])
        S.op("DVE", lambda e: e.reciprocal(out=rstd[:, 0:512], in_=rstd[:, 0:512]), [rr], [rr])
        for cc in range(16):
            t, rt = FP.get()
            TTo(t[:, 0:512], cv[:, cc, :], mean[:, 0:512], ALU.subtract, R_cv(cc) + [rm], [rt])
            TTo(t[:, 0:512], t[:, 0:512], rstd[:, 0:512], ALU.mult, [rt, rr], [rt])
            ACT(yT[:, cc, :], t[:, 0:512], AF.Silu, [rt, R_cvec], [R_yT[cc]], bias=cvc(C_LNB + cc), scale=cvc(C_LNG + cc))

    def ssq_cols(src_ap, src_res, acc, racc):
        sq, rsq = FP.get()
        ACT(sq[:, 0:512], src_ap, AF.Square, src_res, [rsq])
        for tt in range(4):
            MM(banks[6][:, 320 + tt:321 + tt], sq[:, tt * 128:(tt + 1) * 128], onesc[:, :], start=True, stop=True,
               reads=[rsq, R_ones], writes=[R_bank[6]], inc=(tt == 3))
        TTo(acc, acc, banks[6][:, 320:324], ALU.add, [racc, R_bank[6]], [racc])

    def pointwise_phase():
        acc, racc = sm("ssqc", 16, 4)
        S.op("DVE", lambda e: e.memset(acc, 0.0), [], [racc])
        sT_rhs = lambda k: yT[:, k, :]
        sT_res = lambda k: [R_yT[k]]
        for og in range(4):
            proj_group(w_pw, og * 512, 4, sT_rhs, sT_res)
            for j in range(4):
                oc = og * 4 + j
                ACT(f4[:, j, 0:512], banks[j][:, :], AF.Identity, [R_bank[j], R_cvec], [R_f4[j]], bias=cvc(C_BPW + oc))
                ssq_cols(f4[:, j, 0:512], [R_f4[j]], acc, racc)
            proj_group(w_in, 12288 + og * 512, 8, xn_rhs, R_xn)
            for j in range(4):
                oc = og * 4 + j
                sz, rz = FP.get()
                ACT(sz[:, 0:512], banks[j][:, :], AF.Silu, [R_bank[j]], [rz])
                STT(yT[:, 16 + oc, :], f4[:, j, 0:512], cvc(C_COG + oc), sz[:, 0:512], ALU.mult, ALU.mult,
                    [R_f4[j], R_cvec, rz], [R_yT[16 + oc]])
        rc, rrc = sm("rstdc", 20, 4)
        rstd_from(acc, rc, 4, [racc], [rrc], 1.0 / AW)
        return rc, rrc

    def kv_group(hg, newslots):
        proj_group(w_in, 2048 + hg * 512, 8, xn_rhs, R_xn)
        for j in range(4):
            CP("ACT" if j % 2 else "DVE", kts[:, newslots[j], :], banks[j][:, :], [R_bank[j]], [R_kt[newslots[j]]])
        proj_group(w_in, 4096 + hg * 512, 8, xn_rhs, R_xn)
        for j in range(4):
            vt, rv = BP.get()
            CP("ACT" if j % 2 else "DVE", vt[:, 0:512], banks[j][:, :], [R_bank[j]], [rv])
            b6 = banks[6][:, 0:256].bitcast(BF16)
            for blk in range(4):
                TR(b6[:, blk * 128:(blk + 1) * 128], vt[:, blk * 128:(blk + 1) * 128], identb[:, :],
                   [rv, R_identb], [R_bank[6]], inc=(blk == 3))
            CP("ACT" if j % 2 == 0 else "DVE", vts[:, newslots[j], :], b6, [R_bank[6]], [R_vt[newslots[j]]])

    def attn_phase(first_pass):
        acc, racc = sm("ssqa", 24, 4)
        S.op("DVE", lambda e: e.memset(acc, 0.0), [], [racc])
        for hg in range(4):
            newslots = [free.pop(0) for _ in range(4)]
            proj_group(w_in, hg * 512, 8, xn_rhs, R_xn)
            for j in range(4):
                ACT(qT[:, j, :], banks[j][:, :], AF.Copy, [R_bank[j]], [R_qT[j]], scale=float(128 ** -0.5))
            kv_group(hg, newslots)
            proj_group(w_in, 6144 + hg * 512, 8, xn_rhs, R_xn)
            for j in range(4):
                ACT(f4[:, j, 0:512], banks[j][:, :], AF.Silu, [R_bank[j]], [R_f4[j]])
            state = {}

            def stageA(j, pi):
                h = hg * 4 + j
                hs, ns = hist[h], newslots[j]
                if pi == 0:
                    bias, rb = LP.get()
                    DMA("SP", bias[:, :], bias_d[h, :, :], f"bias{LP.i % 2}", [], [rb])
                    state[("bias", j)] = (bias, rb)
                bias, rb = state[("bias", j)]
                nh = 512 - 128 * pi
                ncur = 128 * (pi + 1)
                lq = qT[:, j, pi * 128:(pi + 1) * 128]
                MM(banks[4][:, 0:nh], lq, kts[:, hs, 128 * pi:512], True, True, [R_qT[j], R_kt[hs]], [R_bank[4]])
                MM(banks[5][:, 0:ncur], lq, kts[:, ns, 0:ncur], True, True, [R_qT[j], R_kt[ns]], [R_bank[5]])
                ssb, rss = FP.get()
                if first_pass:
                    STT(ssb[:, 0:nh], banks[4][:, 0:nh], cvc(C_HM), bias[:, 0:nh], ALU.add, ALU.add,
                        [R_bank[4], rb, R_cvec], [rss])
                else:
                    TTo(ssb[:, 0:nh], banks[4][:, 0:nh], bias[:, 0:nh], ALU.add, [R_bank[4], rb], [rss])
                TTo(ssb[:, nh:640], banks[5][:, 0:ncur], bias[:, nh:640], ALU.add, [R_bank[5], rb], [rss])
                u = (j * 4 + pi) % 2
                mx, rmx = sm(f"mx{u}", 28 + u, 1)
                S.op("DVE", lambda e, ssb=ssb, mx=mx: e.tensor_reduce(out=mx, in_=ssb[:, :], axis=AX.X, op=ALU.max,
                                                                     negate=True), [rss], [rmx])
                ex, rex = FP.get()
                rsum, rrs = sm(f"rsum{u}", 30 + u, 1)
                ACT(ex[:, :], ssb[:, :], AF.Exp, [rss, rmx], [rex, rrs], bias=mx, accum=rsum)
                state[("ex", j, pi)] = (ex, rex, rsum, rrs)

            def stageA2(j, pi):
                ex, rex, rsum, rrs = state.pop(("ex", j, pi))
                S.op("DVE", lambda e, rsum=rsum: e.reciprocal(out=rsum, in_=rsum), [rrs], [rrs])
                pn, rpn = BP.get()
                TS(pn[:, :], ex[:, :], rsum, ALU.mult, [rex, rrs], [rpn])
                state[("pn", j, pi)] = (pn, rpn)

            def stageB(j, pi):
                h = hg * 4 + j
                hs, ns = hist[h], newslots[j]
                pn, rpn = state.pop(("pn", j, pi))
                b6 = banks[6][:, 0:320].bitcast(BF16)
                for b in range(5):
                    TR(b6[:, b * 128:(b + 1) * 128], pn[:, b * 128:(b + 1) * 128], identb[:, :],
                       [rpn, R_identb], [R_bank[6]], inc=(b == 4))
                pt, rpt = BP.get()
                CP("ACT", pt[:, :], b6, [R_bank[6]], [rpt])
                for b in range(5):
                    kb = pi + b
                    if kb < 4:
                        vblk, rvv = vts[:, hs, kb * 128:(kb + 1) * 128], R_vt[hs]
                    else:
                        vblk, rvv = vts[:, ns, (kb - 4) * 128:(kb - 3) * 128], R_vt[ns]
                    MM(banks[7][:, pi * 128:(pi + 1) * 128], vblk, pt[:, b * 128:(b + 1) * 128],
                       start=(b == 0), stop=(b == 4), reads=[rvv, rpt], writes=[R_bank[7]], inc=(b == 4))
                if pi == 3:
                    ssq_cols(banks[7][:, :], [R_bank[7]], acc, racc)
                    STT(yT[:, h, :], banks[7][:, :], cvc(C_AOG + h), f4[:, j, 0:512], ALU.mult, ALU.mult,
                        [R_bank[7], R_cvec, R_f4[j]], [R_yT[h]])

            units = [(j, pi) for j in range(4) for pi in range(4)]
            if OPT_PIPE:
                for ui, (j, pi) in enumerate(units):
                    stageA(j, pi)
                    if ui > 0:
                        stageA2(*units[ui - 1])
                        stageB(*units[ui - 1])
                stageA2(*units[-1])
                stageB(*units[-1])
            else:
                for (j, pi) in units:
                    stageA(j, pi)
                    stageA2(j, pi)
                    stageB(j, pi)
            for j in range(4):
                h = hg * 4 + j
                free.append(hist[h])
                hist[h] = newslots[j]
        ra, rra = sm("rstda", 32, 4)
        rstd_from(acc, ra, 4, [racc], [rra], 1.0 / AW)
        return ra, rra

    def halo_kv():
        for hg in range(4):
            newslots = [hist[hg * 4 + j] for j in range(4)]
            kv_group(hg, newslots)

    xrc = {"i": 0}
    outc = {"i": 0}

    def out_phase(t0x, t0y, ra, rra, rc, rrc):
        for cb in range(8):
            for tt in range(4):
                si = xrc["i"] % 8
                xrc["i"] += 1
                DMA("SP", hview[:, tt, cb * 512:(cb + 1) * 512],
                    xc[t0x + tt * 128:t0x + (tt + 1) * 128, cb * 512:(cb + 1) * 512], f"xr{si}", [], R_h(tt, cb))
            for g in range(8):
                slot, rs = wload(w_out[g * 512:(g + 1) * 512, cb * 512:(cb + 1) * 512])
                for kc in range(NKS):
                    k = g * NKS + kc
                    for tt in range(4):
                        bk = (0 if g < 4 else 4) + tt
                        last = k in (15, 31)
                        MM(banks[bk][:, :], yT[:, k, tt * 128:(tt + 1) * 128], slot[:, kc, :], start=(k in (0, 16)),
                           stop=last, reads=[rs, R_yT[k]], writes=[R_bank[bk]], inc=(last or (kc == NKS - 1 and tt == 3)))
                if g == 3:
                    for tt in range(4):
                        hv = hview[:, tt, cb * 512:(cb + 1) * 512]
                        STT(hv, banks[tt][:, :], ra[:, tt:tt + 1], hv, ALU.mult, ALU.add,
                            [R_bank[tt], rra] + R_h(tt, cb), R_h(tt, cb))
                if g == 7:
                    for tt in range(4):
                        hv = hview[:, tt, cb * 512:(cb + 1) * 512]
                        STT(hv, banks[4 + tt][:, :], rc[:, tt:tt + 1], hv, ALU.mult, ALU.add,
                            [R_bank[4 + tt], rrc] + R_h(tt, cb), R_h(tt, cb))
        rh, rrh = sm("rstdh", 36, 4)
        for tt in range(4):
            rms_rows(lambda c, tt=tt: hview[:, tt, c * 512:(c + 1) * 512], R_hrow(tt), (rh[:, tt:tt + 1], rrh), 1.0 / D)
        for tt in range(4):
            for k4 in range(8):
                bk = 6 + (k4 % 2)
                for j in range(4):
                    kc = k4 * 4 + j
                    TR(banks[bk][:, j * 128:(j + 1) * 128], hview[:, tt, kc * 128:(kc + 1) * 128], ident[:, :],
                       R_h(tt, kc // 4) + [R_ident], [R_bank[bk]], inc=(j == 3))
                for j in range(4):
                    kc = k4 * 4 + j
                    TS(yT[:, kc, tt * 128:(tt + 1) * 128], banks[bk][:, j * 128:(j + 1) * 128], cvc(C_PNG + kc),
                       ALU.mult, [R_bank[bk], R_cvec], [R_yT[kc]])
        for tt in range(4):
            ptile, rp = FP.get()
            DMA("SP", ptile[:, 0:256], pc[t0y + tt * 128:t0y + (tt + 1) * 128, :], f"p{tt % 2}", [], [rp])
            for k2 in range(2):
                TR(banks[5][:, k2 * 128:(k2 + 1) * 128], ptile[:, k2 * 128:(k2 + 1) * 128], ident[:, :],
                   [rp, R_ident], [R_bank[5]], inc=(k2 == 1))
            for k2 in range(2):
                CP("ACT", qT[:, k2, tt * 128:(tt + 1) * 128], banks[5][:, k2 * 128:(k2 + 1) * 128], [R_bank[5]], [R_qT[k2]])
        for cb in range(8):
            bbc, rbb = LP.get()
            DMA("SP", bbc[:, 0:512], bpg_d[0:1, cb * 512:(cb + 1) * 512].partition_broadcast(128), f"bb{cb % 2}", [], [rbb])
            pslot, rps = wload(w_ple[:, cb * 512:(cb + 1) * 512], nk=2)
            for tt in range(4):
                for k2 in range(2):
                    MM(banks[4 + tt][:, :], qT[:, k2, tt * 128:(tt + 1) * 128], pslot[:, k2, :], start=(k2 == 0),
                       stop=(k2 == 1), reads=[rps, R_qT[k2]], writes=[R_bank[4 + tt]], inc=(k2 == 1))
            for g in range(8):
                slot, rs = wload(w_pg[g * 512:(g + 1) * 512, cb * 512:(cb + 1) * 512])
                for kc in range(NKS):
                    k = g * NKS + kc
                    for tt in range(4):
                        last = (k == 31)
                        MM(banks[tt][:, :], yT[:, k, tt * 128:(tt + 1) * 128], slot[:, kc, :], start=(k == 0), stop=last,
                           reads=[rs, R_yT[k]], writes=[R_bank[tt]], inc=(last or (kc == NKS - 1 and tt == 3)))
            ts_ = []
            for tt in range(4):
                t, rt = FP.get()
                ts_.append((t, rt))
                STT(t[:, 0:512], banks[tt][:, :], rh[:, tt:tt + 1], bbc[:, 0:512], ALU.mult, ALU.add,
                    [R_bank[tt], rrh, rbb], [rt])
            for tt in range(4):
                t, rt = ts_[tt]
                ACT(t[:, 0:512], t[:, 0:512], AF.Sigmoid, [rt], [rt])
            for tt in range(4):
                t, rt = ts_[tt]
                TTo(t[:, 0:512], t[:, 0:512], banks[4 + tt][:, :], ALU.mult, [rt, R_bank[4 + tt]], [rt])
                hv = hview[:, tt, cb * 512:(cb + 1) * 512]
                TTo(hv, hv, t[:, 0:512], ALU.add, R_h(tt, cb) + [rt], R_h(tt, cb))
        rf, rrf = sm("rstdf", 40, 4)
        for tt in range(4):
            rms_rows(lambda c, tt=tt: hview[:, tt, c * 512:(c + 1) * 512], R_hrow(tt), (rf[:, tt:tt + 1], rrf), 1.0 / D)
        for cb in range(8):
            fgb, rfg = LP.get()
            DMA("SP", fgb[:, 0:512], fg_d[0:1, cb * 512:(cb + 1) * 512].partition_broadcast(128), f"fg{cb % 2}", [], [rfg])
            for tt in range(4):
                o, ro = FP.get()
                STT(o[:, 0:512], hview[:, tt, cb * 512:(cb + 1) * 512], rf[:, tt:tt + 1], fgb[:, 0:512], ALU.mult, ALU.mult,
                    R_h(tt, cb) + [rrf, rfg], [ro])
                si = outc["i"] % 4
                outc["i"] += 1
                DMA("SP", y[t0y + tt * 128:t0y + (tt + 1) * 128, cb * 512:(cb + 1) * 512], o[:, 0:512], f"o{si}", [ro], [])

    if HALO:
        phase0(0)
        for k in range(32):
            CP("ACT" if k % 2 else "DVE", xh[:, k, :], xnT[:, k, 480:512], R_xn(k), [R_xh])
        halo_kv()
    else:
        S.op("DVE", lambda e: e.memset(xh[:, :, :], 0.0), [], [R_xh])
    for pp in range(NPASS):
        t0y = pp * TT
        t0x = t0y + (TT if HALO else 0)
        phase0(t0x)
        conv_phase(pp == 0 and OPT_XH)
        rc, rrc = pointwise_phase()
        ra, rra = attn_phase(pp == 0)
        out_phase(t0x, t0y, ra, rra, rc, rrc)
    S.final_wait_all("SP")

    sem_names = set(ENG) | set(S.dcnt.keys())
    sems = {}
    import contextlib
    with contextlib.ExitStack() as st:
        for n in sorted(sem_names):
            sems[n] = st.enter_context(nc.semaphore("s_" + n))
        block = st.enter_context(nc.Block())

        def emit(eng_obj, name):
            for item in S.prog[name]:
                if item[0] == "wait":
                    eng_obj.wait_ge(sems[item[1]], item[2])
                else:
                    ins = item[1](eng_obj)
                    if item[3]:
                        ins.then_inc(sems[item[2]], item[3])

        @block.tensor
        def _(e):
            emit(e, "PE")

        @block.scalar
        def _(e):
            emit(e, "ACT")

        @block.vector
        def _(e):
            emit(e, "DVE")

        @block.gpsimd
        def _(e):
            emit(e, "POOL")

        @block.sync
        def _(e):
            emit(e, "SP")
    return nc, S


def host_prep(inp):
    f = np.float32
    g = lambda k: np.asarray(inp[k], dtype=f)
    col = lambda v, n: np.ascontiguousarray(v.reshape(n, 128).T)
    cvec = np.zeros((128, NCV), f)
    cvec[:, C_NING:C_NING + 32] = col(g("norm_in_g")[0], 32)
    cvec[:, C_PNG:C_PNG + 32] = col(g("ple_norm_g")[0], 32)
    cvec[:, C_BDW:C_BDW + 16] = col(g("b_dw")[0], 16)
    cvec[:, C_LNG:C_LNG + 16] = col(g("conv_ln_g")[0], 16)
    cvec[:, C_LNB:C_LNB + 16] = col(g("conv_ln_b")[0], 16)
    cvec[:, C_BPW:C_BPW + 16] = col(g("b_pw")[0], 16)
    cvec[:, C_AOG:C_AOG + 16] = col(g("attn_out_g")[0], 16)
    cvec[:, C_COG:C_COG + 16] = col(g("conv_out_g")[0], 16)
    wdw = g("w_dw")[0].reshape(31, 16, 128).transpose(2, 1, 0)
    cvec[:, C_WDW:C_WDW + 496] = wdw.reshape(128, 496)
    q = np.arange(128)[:, None]
    j = np.arange(640)[None, :]
    idx = np.clip(512 + q - j, -256, 256) + 256
    tab = g("rel_table")[0]
    bias = tab[:, idx]
    qc, kc = q // 64, j // 64
    valid = (kc >= qc) & (kc <= 8 + qc)
    bias = np.where(valid[None], bias, f(NEG)).astype(f)
    wd = g("w_dw")[0]
    wdg = np.zeros((16, 128, 32, 128), f)
    ar = np.arange(128)
    for cc in range(16):
        wdg[cc, ar, :31, ar] = wd[:, cc * 128:(cc + 1) * 128].T
    shared = {
        "wdg": wdg.reshape(16 * 128, 4096),
        "w_in": np.ascontiguousarray(g("w_in")[0]), "w_pw": np.ascontiguousarray(g("w_pw")[0]),
        "w_out": np.ascontiguousarray(g("w_out")[0]), "w_pg": np.ascontiguousarray(g("w_ple_gate")[0]),
        "w_ple": np.ascontiguousarray(g("w_ple")[0]), "cvec": cvec,
        "bpg": np.ascontiguousarray(g("b_ple_gate")[0][None, :]), "fg": np.ascontiguousarray(g("final_g")[None, :]),
        "bias": np.ascontiguousarray(bias), "ident": np.eye(128, dtype=f),
    }
    return shared


_CACHE = {}


def kernel(**inp):
    x = np.asarray(inp["x"], dtype=np.float32)
    p = np.asarray(inp["p"], dtype=np.float32)[0]
    B, T, _ = x.shape
    shared = host_prep(inp)
    if "nc" not in _CACHE:
        _CACHE["nc"] = build(4, True)[0]
    nc = _CACHE["nc"]
    in_maps = []
    for c in range(8):
        b, half = c // 2, c % 2
        s0 = half * 2048
        xcore = np.zeros((2560, D), np.float32)
        if half == 1:
            xcore[0:512] = x[b, s0 - 512:s0]
        xcore[512:] = x[b, s0:s0 + 2048]
        m = dict(shared)
        m["xc"] = xcore
        m["pc"] = np.ascontiguousarray(p[b, s0:s0 + 2048])
        cv2 = shared["cvec"].copy()
        cv2[:, C_HM] = 0.0 if half == 1 else NEG
        m["cvec"] = cv2
        in_maps.append(m)
    res = run_bass_kernel_spmd(nc, in_maps, core_ids=list(range(8)))
    out = np.zeros((B, T, D), np.float32)
    for c in range(8):
        b, half = c // 2, c % 2
        out[b, half * 2048:(half + 1) * 2048] = res.results[c]["y"]
    return out
```

```python
import numpy as np
import concourse.bass as bass
import concourse.mybir as mybir
from concourse.bass_utils import run_bass_kernel_spmd

F32 = mybir.dt.float32
BF16 = mybir.dt.bfloat16
AF = mybir.ActivationFunctionType
ALU = mybir.AluOpType
AX = mybir.AxisListType

D = 4096
AW = 2048
NH = 16
TT = 512
EPS = 1e-6
NEG = -1e30
NSLOT = 6
NKS = 4
NCV = 160 + 16 * 31 + 1
C_HM = 160 + 16 * 31
C_NING, C_PNG, C_BDW, C_LNG, C_LNB, C_BPW, C_AOG, C_COG, C_WDW = 0, 32, 64, 80, 96, 112, 128, 144, 160
ENG = ("PE", "ACT", "DVE", "POOL", "SP")
import os
OPT_XH = os.environ.get('KOPT_XH', '1') == '1'
OPT_PIPE = os.environ.get('KOPT_PIPE', '1') == '1'
OPT_P0ACT = os.environ.get('KOPT_P0ACT', '0') == '1'
OPT_P5 = os.environ.get('KOPT_P5', '1') == '1'


class Res:
    __slots__ = ("name", "w", "r")

    def __init__(self, name):
        self.name = name
        self.w = None
        self.r = {}


class Sched:
    def __init__(self):
        self.prog = {e: [] for e in ENG}
        self.cnt = {e: 0 for e in ENG}
        self.waited = {e: {} for e in ENG}
        self.dcnt = {}
        self.nins = 0

    def _wait(self, eng, dep):
        key, val, tag = dep
        if tag == eng and eng == "PE":
            return
        if self.waited[eng].get(key, 0) >= val:
            return
        self.prog[eng].append(("wait", key, val))
        self.waited[eng][key] = val

    def _deps(self, eng, reads, writes):
        for r in reads:
            if r.w is not None:
                self._wait(eng, r.w)
        for w in writes:
            if w.w is not None:
                self._wait(eng, w.w)
            for k, (v, t) in w.r.items():
                self._wait(eng, (k, v, t))

    def op(self, eng, fn, reads=(), writes=(), inc=True):
        self._deps(eng, reads, writes)
        if inc:
            self.cnt[eng] += 1
            val = self.cnt[eng]
            self.prog[eng].append(("op", fn, eng, 1))
        else:
            val = self.cnt[eng] + 1
            self.prog[eng].append(("op", fn, None, 0))
        self.nins += 1
        for r in reads:
            r.r[eng] = (val, eng)
        for w in writes:
            w.w = (eng, val, eng)
            w.r = {}

    def dma(self, q, fn, sem, reads=(), writes=()):
        prev = self.dcnt.get(sem, 0)
        if prev:
            self._wait(q, (sem, prev, "dma"))
        self._deps(q, reads, writes)
        val = prev + 16
        self.dcnt[sem] = val
        self.prog[q].append(("op", fn, sem, 16))
        self.nins += 1
        for r in reads:
            r.r[sem] = (val, "dma")
        for w in writes:
            w.w = (sem, val, "dma")
            w.r = {}

    def final_wait_all(self, eng):
        for sem, val in self.dcnt.items():
            self._wait(eng, (sem, val, "dma"))


class Pool:
    def __init__(self, bufs):
        self.bufs = bufs
        self.i = 0

    def get(self):
        b = self.bufs[self.i % len(self.bufs)]
        self.i += 1
        return b


def build(NPASS=4, HALO=True):
    nc = bass.Bass("TRN2", target_bir_lowering=False)
    NTX = (TT if HALO else 0) + TT * NPASS
    NTY = TT * NPASS
    dr = lambda n, s: nc.dram_tensor(n, s, F32, kind="ExternalInput").ap()
    xc = dr("xc", [NTX, D])
    pc = dr("pc", [NTY, 256])
    w_in = dr("w_in", [D, 14336])
    w_pw = dr("w_pw", [AW, AW])
    w_out = dr("w_out", [D, D])
    w_pg = dr("w_pg", [D, D])
    w_ple = dr("w_ple", [256, D])
    cvec_d = dr("cvec", [128, NCV])
    bpg_d = dr("bpg", [1, D])
    fg_d = dr("fg", [1, D])
    bias_d = dr("bias", [NH, 128, 640])
    ident_d = dr("ident", [128, 128])
    wdg = dr("wdg", [16 * 128, 4096])
    y = nc.dram_tensor("y", [NTY, D], F32, kind="ExternalOutput").ap()

    S = Sched()
    sb = lambda n, s, dt=F32: nc.alloc_sbuf_tensor(n, s, dt)
    kts = sb("kts", [128, 20, 512], BF16)
    vts = sb("vts", [128, 20, 512], BF16)
    wb = sb("wb", [128, NSLOT, NKS * 512], BF16)
    yT = sb("yT", [128, 32, 512], BF16)
    CF = sb("CF", [128, 16384], F32)
    cvec = sb("cvec_sb", [128, NCV], F32)
    ident = sb("ident_sb", [128, 128], F32)
    identb = sb("identb", [128, 128], BF16)
    onesm = sb("onesm", [128, 128], F32)
    onesc = sb("onesc", [128, 1], F32)
    uhist = sb("uhist", [128, 16, 30], F32)
    f4 = sb("f4", [128, 4, 544], F32)
    xh = sb("xh", [128, 32, 32], BF16)
    ub = sb("ub", [128, 4, 544], BF16)
    qT = sb("qT", [128, 4, 512], BF16)
    small = sb("small", [128, 64], F32)
    NF = 4
    lbufs = [sb(f"lb{i}", [128, 640], F32) for i in range(2)]
    fbufs = [sb(f"fb{i}", [128, 640], F32) for i in range(NF)]
    NB = 4
    bbufs = [sb(f"bb{i}", [128, 640], BF16) for i in range(NB)]
    banks = [nc.alloc_psum_tensor(f"bank{i}", [128, 512], F32) for i in range(8)]

    CFb = CF[:, 0:8192].bitcast(BF16)
    xnT = CFb.rearrange("p (k t) -> p k t", t=512)
    cv = CF[:, 8192:16384].rearrange("p (c t) -> p c t", t=512)
    hview = CF[:, :].rearrange("p (a f) -> p a f", f=4096)
    xt_bufs = [CF[:, 8192:12288], CF[:, 12288:16384]]

    R_B = [Res(f"B{i}") for i in range(64)]
    R_xn = lambda kc: [R_B[kc]]
    R_cv = lambda cc: [R_B[32 + 2 * cc], R_B[33 + 2 * cc]]
    R_h = lambda tt, cb: [R_B[tt * 16 + cb * 2], R_B[tt * 16 + cb * 2 + 1]]
    R_hrow = lambda tt: R_B[tt * 16:(tt + 1) * 16]
    R_xt = lambda i: R_B[32 + 16 * i:48 + 16 * i]
    R_kt = [Res(f"kt{i}") for i in range(20)]
    R_vt = [Res(f"vt{i}") for i in range(20)]
    R_wb = [Res(f"wb{i}") for i in range(NSLOT)]
    R_yT = [Res(f"yT{i}") for i in range(32)]
    R_bank = [Res(f"bank{i}") for i in range(8)]
    R_cvec, R_ident, R_identb, R_ones = Res("cvec"), Res("ident"), Res("identb"), Res("ones")
    R_uhist = [Res(f"uh{i}") for i in range(16)]
    R_f4 = [Res(f"f4{i}") for i in range(4)]
    R_xh = Res("xh")
    R_ub = [Res(f"ub{i}") for i in range(4)]
    junk, R_junk = f4[:, 0, 0:512], R_f4[0]
    R_qT = [Res(f"qT{i}") for i in range(4)]
    R_small = {}
    FP = Pool([(fbufs[i], Res(f"fb{i}")) for i in range(NF)])
    BP = Pool([(bbufs[i], Res(f"bb{i}")) for i in range(NB)])
    LP = Pool([(lbufs[i], Res(f"lb{i}")) for i in range(2)])

    def sm(name, idx, n=1):
        if name not in R_small:
            R_small[name] = Res(name)
        return small[:, idx:idx + n], R_small[name]

    cvc = lambda c: cvec[:, c:c + 1]

    def ACT(out, in_, func, reads, writes, bias=None, scale=None, accum=None):
        kw = {}
        if bias is not None:
            kw["bias"] = bias
        if scale is not None:
            kw["scale"] = scale
        if accum is not None:
            kw["accum_out"] = accum
        S.op("ACT", lambda e: e.activation(out=out, in_=in_, func=func, **kw), reads, writes)

    def TS(out, in0, s1, op0, reads, writes, s2=None, op1=None, eng="DVE"):
        if op1 is None:
            S.op(eng, lambda e: e.tensor_scalar(out=out, in0=in0, scalar1=s1, scalar2=None, op0=op0), reads, writes)
        else:
            S.op(eng, lambda e: e.tensor_scalar(out=out, in0=in0, scalar1=s1, scalar2=s2, op0=op0, op1=op1), reads, writes)

    def STT(out, in0, sc, in1, op0, op1, reads, writes):
        S.op("DVE", lambda e: e.scalar_tensor_tensor(out=out, in0=in0, scalar=sc, in1=in1, op0=op0, op1=op1), reads, writes)

    def TTo(out, in0, in1, op, reads, writes, eng="DVE"):
        S.op(eng, lambda e: e.tensor_tensor(out=out, in0=in0, in1=in1, op=op), reads, writes)

    def CP(eng, out, in_, reads, writes):
        if eng == "ACT":
            S.op("ACT", lambda e: e.copy(out=out, in_=in_), reads, writes)
        else:
            S.op(eng, lambda e: e.tensor_copy(out=out, in_=in_), reads, writes)

    def MM(out, lhsT, rhs, start, stop, reads, writes, inc=True):
        S.op("PE", lambda e: e.matmul(out, lhsT, rhs, start=start, stop=stop), reads, writes, inc=inc)

    def TR(out, in_, idn, reads, writes, inc=True):
        S.op("PE", lambda e: e.transpose(out, in_, idn), reads, writes, inc=inc)

    def DMA(q, out, in_, sem, reads, writes):
        S.dma(q, lambda e: e.dma_start(out=out, in_=in_), sem, reads, writes)

    def rstd_from(ssq_ap, out_ap, n, reads, writes, inv):
        TS(out_ap, ssq_ap, inv, ALU.mult, reads, writes, s2=EPS, op1=ALU.add)
        ACT(out_ap, out_ap, AF.Sqrt, writes, writes)
        S.op("DVE", lambda e: e.reciprocal(out=out_ap, in_=out_ap), writes, writes)

    wstate = {"i": 0}

    def wload(src, nk=NKS):
        i = wstate["i"] % NSLOT
        wstate["i"] += 1
        dst = wb[:, i, 0:nk * 512].rearrange("p (k c) -> p k c", c=512)
        DMA("POOL", dst, src.rearrange("(k p) c -> p k c", p=128), f"w{i}", [], [R_wb[i]])
        return dst, R_wb[i]

    def wload_flat(src):
        i = wstate["i"] % NSLOT
        wstate["i"] += 1
        dst = wb[:, i, 0:2048]
        DMA("POOL", dst, src, f"w{i}", [], [R_wb[i]])
        return dst, R_wb[i]

    def proj_group(wmat, c0, nkg, rhs_fn, rhs_res_fn, bset=(0, 1, 2, 3), extra_bank=None):
        for g in range(nkg):
            slot, rs = wload(wmat[g * 512:(g + 1) * 512, c0:c0 + 512])
            for kc in range(NKS):
                k = g * NKS + kc
                for j in range(4):
                    last = (g == nkg - 1 and kc == NKS - 1)
                    if extra_bank is not None:
                        MM(banks[extra_bank][:, j * 32:(j + 1) * 32], slot[:, kc, j * 128:(j + 1) * 128], xh[:, k, :],
                           start=(k == 0 and j == 0), stop=(last and j == 3), reads=[rs, R_xh],
                           writes=[R_bank[extra_bank]], inc=False)
                    MM(banks[bset[j]][:, :], slot[:, kc, j * 128:(j + 1) * 128], rhs_fn(k),
                       start=(k == 0), stop=last, reads=[rs] + rhs_res_fn(k), writes=[R_bank[bset[j]]],
                       inc=(last or (kc == NKS - 1 and j == 3)))

    xn_rhs = lambda k: xnT[:, k, :]

    DMA("SP", cvec[:, :], cvec_d[:, :], "c0", [], [R_cvec])
    DMA("SP", ident[:, :], ident_d[:, :], "c1", [], [R_ident])
    S.op("DVE", lambda e: e.memset(onesm[:, :], 1.0 / 2048.0), [], [R_ones])
    S.op("DVE", lambda e: e.memset(onesc[:, :], 1.0), [], [R_ones])
    S.op("DVE", lambda e: e.tensor_copy(out=identb[:, :], in_=ident[:, :]), [R_ident], [R_identb])
    S.op("DVE", lambda e: e.memset(uhist[:, :, :], 0.0), [], R_uhist)
    for i in range(20):
        S.op("POOL", lambda e, i=i: e.memset(kts[:, i, :], 0.0), [], [R_kt[i]])
        S.op("POOL", lambda e, i=i: e.memset(vts[:, i, :], 0.0), [], [R_vt[i]])

    hist = list(range(16))
    free = [16, 17, 18, 19]
    xtc = {"i": 0}

    def rms_rows(src_fn, src_res, dst_small, inv):
        ss8 = small[:, 0:8]
        r8 = [sm(f"ss8_{c}", c, 1)[1] for c in range(8)]
        for c in range(8):
            ACT(f4[:, c % 4, 0:512], src_fn(c), AF.Square, src_res, [R_f4[c % 4], r8[c]], accum=ss8[:, c:c + 1])
        tot, rt = sm("sstot", 8, 1)
        S.op("DVE", lambda e: e.tensor_reduce(out=tot, in_=ss8, axis=AX.X, op=ALU.add), r8, [rt])
        rstd_from(tot, dst_small[0], 1, [rt], [dst_small[1]], inv)

    def phase0(t0):
        for tt in range(4):
            bi = xtc["i"] % 2
            xtc["i"] += 1
            xt = xt_bufs[bi]
            rx = R_xt(bi)
            DMA("SP", xt, xc[t0 + tt * 128:t0 + (tt + 1) * 128, :], f"x{bi}", [], rx)
            rs = sm("rstdx", 9, 1)
            rms_rows(lambda c: xt[:, c * 512:(c + 1) * 512], rx, rs, 1.0 / D)
            TS(xt, xt, rs[0], ALU.mult, rx + [rs[1]], rx)
            for k4 in range(8):
                bk = 4 + (k4 % 2)
                for j in range(4):
                    kc = k4 * 4 + j
                    TR(banks[bk][:, j * 128:(j + 1) * 128], xt[:, kc * 128:(kc + 1) * 128], ident[:, :],
                       rx + [R_ident], [R_bank[bk]], inc=(j == 3))
                for j in range(4):
                    kc = k4 * 4 + j
                    if j % 2 == 0 or not OPT_P0ACT:
                        TS(xnT[:, kc, tt * 128:(tt + 1) * 128], banks[bk][:, j * 128:(j + 1) * 128], cvc(C_NING + kc),
                           ALU.mult, [R_bank[bk], R_cvec], R_xn(kc))
                    else:
                        ACT(xnT[:, kc, tt * 128:(tt + 1) * 128], banks[bk][:, j * 128:(j + 1) * 128], AF.Copy,
                            [R_bank[bk], R_cvec], R_xn(kc), scale=cvc(C_NING + kc))

    def conv_phase(first):
        for cgp in range(4):
            proj_group(w_in, 10240 + cgp * 512, 8, xn_rhs, R_xn, extra_bank=(6 if first else None))
            for j in range(4):
                ACT(f4[:, j, 0:512], banks[j][:, :], AF.Sigmoid, [R_bank[j]], [R_f4[j]])
                if first:
                    ACT(f4[:, j, 512:542], banks[6][:, j * 32 + 2:j * 32 + 32], AF.Sigmoid, [R_bank[6]], [R_f4[j]])
            proj_group(w_in, 8192 + cgp * 512, 8, xn_rhs, R_xn, extra_bank=(7 if first else None))
            for j in range(4):
                cc = cgp * 4 + j
                TTo(ub[:, j, 30:542], banks[j][:, :], f4[:, j, 0:512], ALU.mult, [R_bank[j], R_f4[j]], [R_ub[j]])
                if first:
                    TTo(ub[:, j, 0:30], banks[7][:, j * 32 + 2:j * 32 + 32], f4[:, j, 512:542], ALU.mult,
                        [R_bank[7], R_f4[j]], [R_ub[j]])
                else:
                    CP("ACT", ub[:, j, 0:30], uhist[:, cc, :], [R_uhist[cc]], [R_ub[j]])
                CP("ACT", uhist[:, cc, :], ub[:, j, 512:542], [R_ub[j]], [R_uhist[cc]])
            for j in range(4):
                cc = cgp * 4 + j
                rc = R_cv(cc)
                for half in range(2):
                    wsl, rws = wload_flat(wdg[cc * 128:(cc + 1) * 128, half * 2048:(half + 1) * 2048])
                    ntap = 16 if half == 0 else 15
                    for t in range(ntap):
                        tap = half * 16 + t
                        MM(banks[j][:, :], wsl[:, t * 128:(t + 1) * 128], ub[:, j, tap:tap + 512],
                           start=(tap == 0), stop=(tap == 30), reads=[rws, R_ub[j]], writes=[R_bank[j]],
                           inc=(t == ntap - 1))
                ACT(cv[:, cc, :], banks[j][:, :], AF.Identity, [R_bank[j], R_cvec], rc, bias=cvc(C_BDW + cc))
                sq, rsq = FP.get()
                ACT(sq[:, 0:512], cv[:, cc, :], AF.Square, rc, [rsq])
                MM(banks[4][:, :], onesm[:, :], cv[:, cc, :], start=(cc == 0), stop=(cc == 15),
                   reads=[R_ones] + rc, writes=[R_bank[4]])
                MM(banks[5][:, :], onesm[:, :], sq[:, 0:512], start=(cc == 0), stop=(cc == 15),
                   reads=[R_ones, rsq], writes=[R_bank[5]])
        mean, rm = LP.get()
        rstd, rr = LP.get()
        CP("DVE", mean[:, 0:512], banks[4][:, :], [R_bank[4]], [rm])
        TTo(rstd[:, 0:512], mean[:, 0:512], mean[:, 0:512], ALU.mult, [rm], [rr])
        TTo(rstd[:, 0:512], banks[5][:, :], rstd[:, 0:512], ALU.subtract, [R_bank[5], rr], [rr])
        TS(rstd[:, 0:512], rstd[:, 0:512], EPS, ALU.add, [rr], [rr])
        ACT(rstd[:, 0:512], rstd[:, 0:512], AF.Sqrt, [rr], [rr])
        S.op("DVE", lambda e: e.reciprocal(out=rstd[:, 0:512], in_=rstd[:, 0:512]), [rr], [rr])
        for cc in range(16):
            t, rt = FP.get()
            TTo(t[:, 0:512], cv[:, cc, :], mean[:, 0:512], ALU.subtract, R_cv(cc) + [rm], [rt])
            TTo(t[:, 0:512], t[:, 0:512], rstd[:, 0:512], ALU.mult, [rt, rr], [rt])
            ACT(yT[:, cc, :], t[:, 0:512], AF.Silu, [rt, R_cvec], [R_yT[cc]], bias=cvc(C_LNB + cc), scale=cvc(C_LNG + cc))

    def ssq_cols(src_ap, src_res, acc, racc):
        sq, rsq = FP.get()
        ACT(sq[:, 0:512], src_ap, AF.Square, src_res, [rsq])
        for tt in range(4):
            MM(banks[6][:, 320 + tt:321 + tt], sq[:, tt * 128:(tt + 1) * 128], onesc[:, :], start=True, stop=True,
               reads=[rsq, R_ones], writes=[R_bank[6]], inc=(tt == 3))
        TTo(acc, acc, banks[6][:, 320:324], ALU.add, [racc, R_bank[6]], [racc])

    def pointwise_phase():
        acc, racc = sm("ssqc", 16, 4)
        S.op("DVE", lambda e: e.memset(acc, 0.0), [], [racc])
        sT_rhs = lambda k: yT[:, k, :]
        sT_res = lambda k: [R_yT[k]]
        for og in range(4):
            proj_group(w_pw, og * 512, 4, sT_rhs, sT_res)
            for j in range(4):
                oc = og * 4 + j
                ACT(f4[:, j, 0:512], banks[j][:, :], AF.Identity, [R_bank[j], R_cvec], [R_f4[j]], bias=cvc(C_BPW + oc))
                ssq_cols(f4[:, j, 0:512], [R_f4[j]], acc, racc)
            proj_group(w_in, 12288 + og * 512, 8, xn_rhs, R_xn)
            for j in range(4):
                oc = og * 4 + j
                sz, rz = FP.get()
                ACT(sz[:, 0:512], banks[j][:, :], AF.Silu, [R_bank[j]], [rz])
                STT(yT[:, 16 + oc, :], f4[:, j, 0:512], cvc(C_COG + oc), sz[:, 0:512], ALU.mult, ALU.mult,
                    [R_f4[j], R_cvec, rz], [R_yT[16 + oc]])
        rc, rrc = sm("rstdc", 20, 4)
        rstd_from(acc, rc, 4, [racc], [rrc], 1.0 / AW)
        return rc, rrc

    def kv_group(hg, newslots):
        proj_group(w_in, 2048 + hg * 512, 8, xn_rhs, R_xn)
        for j in range(4):
            CP("ACT" if j % 2 else "DVE", kts[:, newslots[j], :], banks[j][:, :], [R_bank[j]], [R_kt[newslots[j]]])
        proj_group(w_in, 4096 + hg * 512, 8, xn_rhs, R_xn)
        for j in range(4):
            vt, rv = BP.get()
            CP("ACT" if j % 2 else "DVE", vt[:, 0:512], banks[j][:, :], [R_bank[j]], [rv])
            b6 = banks[6][:, 0:256].bitcast(BF16)
            for blk in range(4):
                TR(b6[:, blk * 128:(blk + 1) * 128], vt[:, blk * 128:(blk + 1) * 128], identb[:, :],
                   [rv, R_identb], [R_bank[6]], inc=(blk == 3))
            CP("ACT" if j % 2 == 0 else "DVE", vts[:, newslots[j], :], b6, [R_bank[6]], [R_vt[newslots[j]]])

    def attn_phase(first_pass):
        acc, racc = sm("ssqa", 24, 4)
        S.op("DVE", lambda e: e.memset(acc, 0.0), [], [racc])
        for hg in range(4):
            newslots = [free.pop(0) for _ in range(4)]
            proj_group(w_in, hg * 512, 8, xn_rhs, R_xn)
            for j in range(4):
                ACT(qT[:, j, :], banks[j][:, :], AF.Copy, [R_bank[j]], [R_qT[j]], scale=float(128 ** -0.5))
            kv_group(hg, newslots)
            proj_group(w_in, 6144 + hg * 512, 8, xn_rhs, R_xn)
            for j in range(4):
                ACT(f4[:, j, 0:512], banks[j][:, :], AF.Silu, [R_bank[j]], [R_f4[j]])
            state = {}

            def stageA(j, pi):
                h = hg * 4 + j
                hs, ns = hist[h], newslots[j]
                if pi == 0:
                    bias, rb = LP.get()
                    DMA("SP", bias[:, :], bias_d[h, :, :], f"bias{LP.i % 2}", [], [rb])
                    state[("bias", j)] = (bias, rb)
                bias, rb = state[("bias", j)]
                nh = 512 - 128 * pi
                ncur = 128 * (pi + 1)
                lq = qT[:, j, pi * 128:(pi + 1) * 128]
                MM(banks[4][:, 0:nh], lq, kts[:, hs, 128 * pi:512], True, True, [R_qT[j], R_kt[hs]], [R_bank[4]])
                MM(banks[5][:, 0:ncur], lq, kts[:, ns, 0:ncur], True, True, [R_qT[j], R_kt[ns]], [R_bank[5]])
                ssb, rss = FP.get()
                if first_pass:
                    STT(ssb[:, 0:nh], banks[4][:, 0:nh], cvc(C_HM), bias[:, 0:nh], ALU.add, ALU.add,
                        [R_bank[4], rb, R_cvec], [rss])
                else:
                    TTo(ssb[:, 0:nh], banks[4][:, 0:nh], bias[:, 0:nh], ALU.add, [R_bank[4], rb], [rss])
                TTo(ssb[:, nh:640], banks[5][:, 0:ncur], bias[:, nh:640], ALU.add, [R_bank[5], rb], [rss])
                u = (j * 4 + pi) % 2
                mx, rmx = sm(f"mx{u}", 28 + u, 1)
                S.op("DVE", lambda e, ssb=ssb, mx=mx: e.tensor_reduce(out=mx, in_=ssb[:, :], axis=AX.X, op=ALU.max,
                                                                     negate=True), [rss], [rmx])
                ex, rex = FP.get()
                rsum, rrs = sm(f"rsum{u}", 30 + u, 1)
                ACT(ex[:, :], ssb[:, :], AF.Exp, [rss, rmx], [rex, rrs], bias=mx, accum=rsum)
                state[("ex", j, pi)] = (ex, rex, rsum, rrs)

            def stageA2(j, pi):
                ex, rex, rsum, rrs = state.pop(("ex", j, pi))
                S.op("DVE", lambda e, rsum=rsum: e.reciprocal(out=rsum, in_=rsum), [rrs], [rrs])
                pn, rpn = BP.get()
                TS(pn[:, :], ex[:, :], rsum, ALU.mult, [rex, rrs], [rpn])
                state[("pn", j, pi)] = (pn, rpn)

            def stageB(j, pi):
                h = hg * 4 + j
                hs, ns = hist[h], newslots[j]
                pn, rpn = state.pop(("pn", j, pi))
                b6 = banks[6][:, 0:320].bitcast(BF16)
                for b in range(5):
                    TR(b6[:, b * 128:(b + 1) * 128], pn[:, b * 128:(b + 1) * 128], identb[:, :],
                       [rpn, R_identb], [R_bank[6]], inc=(b == 4))
                pt, rpt = BP.get()
                CP("ACT", pt[:, :], b6, [R_bank[6]], [rpt])
                for b in range(5):
                    kb = pi + b
                    if kb < 4:
                        vblk, rvv = vts[:, hs, kb * 128:(kb + 1) * 128], R_vt[hs]
                    else:
                        vblk, rvv = vts[:, ns, (kb - 4) * 128:(kb - 3) * 128], R_vt[ns]
                    MM(banks[7][:, pi * 128:(pi + 1) * 128], vblk, pt[:, b * 128:(b + 1) * 128],
                       start=(b == 0), stop=(b == 4), reads=[rvv, rpt], writes=[R_bank[7]], inc=(b == 4))
                if pi == 3:
                    ssq_cols(banks[7][:, :], [R_bank[7]], acc, racc)
                    STT(yT[:, h, :], banks[7][:, :], cvc(C_AOG + h), f4[:, j, 0:512], ALU.mult, ALU.mult,
                        [R_bank[7], R_cvec, R_f4[j]], [R_yT[h]])

            units = [(j, pi) for j in range(4) for pi in range(4)]
            if OPT_PIPE:
                for ui, (j, pi) in enumerate(units):
                    stageA(j, pi)
                    if ui > 0:
                        stageA2(*units[ui - 1])
                        stageB(*units[ui - 1])
                stageA2(*units[-1])
                stageB(*units[-1])
            else:
                for (j, pi) in units:
                    stageA(j, pi)
                    stageA2(j, pi)
                    stageB(j, pi)
            for j in range(4):
                h = hg * 4 + j
                free.append(hist[h])
                hist[h] = newslots[j]
        ra, rra = sm("rstda", 32, 4)
        rstd_from(acc, ra, 4, [racc], [rra], 1.0 / AW)
        return ra, rra

    def halo_kv():
        for hg in range(4):
            newslots = [hist[hg * 4 + j] for j in range(4)]
            kv_group(hg, newslots)

    xrc = {"i": 0}
    outc = {"i": 0}

    def out_phase(t0x, t0y, ra, rra, rc, rrc):
        for cb in range(8):
            for tt in range(4):
                si = xrc["i"] % 8
                xrc["i"] += 1
                DMA("SP", hview[:, tt, cb * 512:(cb + 1) * 512],
                    xc[t0x + tt * 128:t0x + (tt + 1) * 128, cb * 512:(cb + 1) * 512], f"xr{si}", [], R_h(tt, cb))
            for g in range(8):
                slot, rs = wload(w_out[g * 512:(g + 1) * 512, cb * 512:(cb + 1) * 512])
                for kc in range(NKS):
                    k = g * NKS + kc
                    for tt in range(4):
                        bk = (0 if g < 4 else 4) + tt
                        last = k in (15, 31)
                        MM(banks[bk][:, :], yT[:, k, tt * 128:(tt + 1) * 128], slot[:, kc, :], start=(k in (0, 16)),
                           stop=last, reads=[rs, R_yT[k]], writes=[R_bank[bk]], inc=(last or (kc == NKS - 1 and tt == 3)))
                if g == 3:
                    for tt in range(4):
                        hv = hview[:, tt, cb * 512:(cb + 1) * 512]
                        STT(hv, banks[tt][:, :], ra[:, tt:tt + 1], hv, ALU.mult, ALU.add,
                            [R_bank[tt], rra] + R_h(tt, cb), R_h(tt, cb))
                if g == 7:
                    for tt in range(4):
                        hv = hview[:, tt, cb * 512:(cb + 1) * 512]
                        STT(hv, banks[4 + tt][:, :], rc[:, tt:tt + 1], hv, ALU.mult, ALU.add,
                            [R_bank[4 + tt], rrc] + R_h(tt, cb), R_h(tt, cb))
        rh, rrh = sm("rstdh", 36, 4)
        for tt in range(4):
            rms_rows(lambda c, tt=tt: hview[:, tt, c * 512:(c + 1) * 512], R_hrow(tt), (rh[:, tt:tt + 1], rrh), 1.0 / D)
        for tt in range(4):
            for k4 in range(8):
                bk = 6 + (k4 % 2)
                for j in range(4):
                    kc = k4 * 4 + j
                    TR(banks[bk][:, j * 128:(j + 1) * 128], hview[:, tt, kc * 128:(kc + 1) * 128], ident[:, :],
                       R_h(tt, kc // 4) + [R_ident], [R_bank[bk]], inc=(j == 3))
                for j in range(4):
                    kc = k4 * 4 + j
                    TS(yT[:, kc, tt * 128:(tt + 1) * 128], banks[bk][:, j * 128:(j + 1) * 128], cvc(C_PNG + kc),
                       ALU.mult, [R_bank[bk], R_cvec], [R_yT[kc]])
        for tt in range(4):
            ptile, rp = FP.get()
            DMA("SP", ptile[:, 0:256], pc[t0y + tt * 128:t0y + (tt + 1) * 128, :], f"p{tt % 2}", [], [rp])
            for k2 in range(2):
                TR(banks[5][:, k2 * 128:(k2 + 1) * 128], ptile[:, k2 * 128:(k2 + 1) * 128], ident[:, :],
                   [rp, R_ident], [R_bank[5]], inc=(k2 == 1))
            for k2 in range(2):
                CP("ACT", qT[:, k2, tt * 128:(tt + 1) * 128], banks[5][:, k2 * 128:(k2 + 1) * 128], [R_bank[5]], [R_qT[k2]])
        for cb in range(8):
            bbc, rbb = LP.get()
            DMA("SP", bbc[:, 0:512], bpg_d[0:1, cb * 512:(cb + 1) * 512].partition_broadcast(128), f"bb{cb % 2}", [], [rbb])
            for g in range(8):
                slot, rs = wload(w_pg[g * 512:(g + 1) * 512, cb * 512:(cb + 1) * 512])
                for kc in range(NKS):
                    k = g * NKS + kc
                    for tt in range(4):
                        last = (k == 31)
                        MM(banks[tt][:, :], yT[:, k, tt * 128:(tt + 1) * 128], slot[:, kc, :], start=(k == 0), stop=last,
                           reads=[rs, R_yT[k]], writes=[R_bank[tt]], inc=(last or (kc == NKS - 1 and tt == 3)))
            pslot, rps = wload(w_ple[:, cb * 512:(cb + 1) * 512], nk=2)
            for tt in range(4):
                for k2 in range(2):
                    MM(banks[4 + tt][:, :], qT[:, k2, tt * 128:(tt + 1) * 128], pslot[:, k2, :], start=(k2 == 0),
                       stop=(k2 == 1), reads=[rps, R_qT[k2]], writes=[R_bank[4 + tt]], inc=(k2 == 1))
            ts_ = []
            for tt in range(4):
                t, rt = FP.get()
                ts_.append((t, rt))
                STT(t[:, 0:512], banks[tt][:, :], rh[:, tt:tt + 1], bbc[:, 0:512], ALU.mult, ALU.add,
                    [R_bank[tt], rrh, rbb], [rt])
            for tt in range(4):
                t, rt = ts_[tt]
                ACT(t[:, 0:512], t[:, 0:512], AF.Sigmoid, [rt], [rt])
            for tt in range(4):
                t, rt = ts_[tt]
                TTo(t[:, 0:512], t[:, 0:512], banks[4 + tt][:, :], ALU.mult, [rt, R_bank[4 + tt]], [rt])
                hv = hview[:, tt, cb * 512:(cb + 1) * 512]
                TTo(hv, hv, t[:, 0:512], ALU.add, R_h(tt, cb) + [rt], R_h(tt, cb))
        rf, rrf = sm("rstdf", 40, 4)
        for tt in range(4):
            rms_rows(lambda c, tt=tt: hview[:, tt, c * 512:(c + 1) * 512], R_hrow(tt), (rf[:, tt:tt + 1], rrf), 1.0 / D)
        for cb in range(8):
            fgb, rfg = LP.get()
            DMA("SP", fgb[:, 0:512], fg_d[0:1, cb * 512:(cb + 1) * 512].partition_broadcast(128), f"fg{cb % 2}", [], [rfg])
            for tt in range(4):
                o, ro = FP.get()
                STT(o[:, 0:512], hview[:, tt, cb * 512:(cb + 1) * 512], rf[:, tt:tt + 1], fgb[:, 0:512], ALU.mult, ALU.mult,
                    R_h(tt, cb) + [rrf, rfg], [ro])
                si = outc["i"] % 4
                outc["i"] += 1
                DMA("SP", y[t0y + tt * 128:t0y + (tt + 1) * 128, cb * 512:(cb + 1) * 512], o[:, 0:512], f"o{si}", [ro], [])

    if HALO:
        phase0(0)
        for k in range(32):
            CP("ACT" if k % 2 else "DVE", xh[:, k, :], xnT[:, k, 480:512], R_xn(k), [R_xh])
        halo_kv()
    else:
        S.op("DVE", lambda e: e.memset(xh[:, :, :], 0.0), [], [R_xh])
    for pp in range(NPASS):
        t0y = pp * TT
        t0x = t0y + (TT if HALO else 0)
        phase0(t0x)
        conv_phase(pp == 0 and OPT_XH)
        rc, rrc = pointwise_phase()
        ra, rra = attn_phase(pp == 0)
        out_phase(t0x, t0y, ra, rra, rc, rrc)
    S.final_wait_all("SP")

    sem_names = set(ENG) | set(S.dcnt.keys())
    sems = {}
    import contextlib
    with contextlib.ExitStack() as st:
        for n in sorted(sem_names):
            sems[n] = st.enter_context(nc.semaphore("s_" + n))
        block = st.enter_context(nc.Block())

        def emit(eng_obj, name):
            for item in S.prog[name]:
                if item[0] == "wait":
                    eng_obj.wait_ge(sems[item[1]], item[2])
                else:
                    ins = item[1](eng_obj)
                    if item[3]:
                        ins.then_inc(sems[item[2]], item[3])

        @block.tensor
        def _(e):
            emit(e, "PE")

        @block.scalar
        def _(e):
            emit(e, "ACT")

        @block.vector
        def _(e):
            emit(e, "DVE")

        @block.gpsimd
        def _(e):
            emit(e, "POOL")

        @block.sync
        def _(e):
            emit(e, "SP")
    return nc, S


def host_prep(inp):
    f = np.float32
    g = lambda k: np.asarray(inp[k], dtype=f)
    col = lambda v, n: np.ascontiguousarray(v.reshape(n, 128).T)
    cvec = np.zeros((128, NCV), f)
    cvec[:, C_NING:C_NING + 32] = col(g("norm_in_g")[0], 32)
    cvec[:, C_PNG:C_PNG + 32] = col(g("ple_norm_g")[0], 32)
    cvec[:, C_BDW:C_BDW + 16] = col(g("b_dw")[0], 16)
    cvec[:, C_LNG:C_LNG + 16] = col(g("conv_ln_g")[0], 16)
    cvec[:, C_LNB:C_LNB + 16] = col(g("conv_ln_b")[0], 16)
    cvec[:, C_BPW:C_BPW + 16] = col(g("b_pw")[0], 16)
    cvec[:, C_AOG:C_AOG + 16] = col(g("attn_out_g")[0], 16)
    cvec[:, C_COG:C_COG + 16] = col(g("conv_out_g")[0], 16)
    wdw = g("w_dw")[0].reshape(31, 16, 128).transpose(2, 1, 0)
    cvec[:, C_WDW:C_WDW + 496] = wdw.reshape(128, 496)
    q = np.arange(128)[:, None]
    j = np.arange(640)[None, :]
    idx = np.clip(512 + q - j, -256, 256) + 256
    tab = g("rel_table")[0]
    bias = tab[:, idx]
    qc, kc = q // 64, j // 64
    valid = (kc >= qc) & (kc <= 8 + qc)
    bias = np.where(valid[None], bias, f(NEG)).astype(f)
    wd = g("w_dw")[0]
    wdg = np.zeros((16, 128, 32, 128), f)
    ar = np.arange(128)
    for cc in range(16):
        wdg[cc, ar, :31, ar] = wd[:, cc * 128:(cc + 1) * 128].T
    shared = {
        "wdg": wdg.reshape(16 * 128, 4096),
        "w_in": np.ascontiguousarray(g("w_in")[0]), "w_pw": np.ascontiguousarray(g("w_pw")[0]),
        "w_out": np.ascontiguousarray(g("w_out")[0]), "w_pg": np.ascontiguousarray(g("w_ple_gate")[0]),
        "w_ple": np.ascontiguousarray(g("w_ple")[0]), "cvec": cvec,
        "bpg": np.ascontiguousarray(g("b_ple_gate")[0][None, :]), "fg": np.ascontiguousarray(g("final_g")[None, :]),
        "bias": np.ascontiguousarray(bias), "ident": np.eye(128, dtype=f),
    }
    return shared


_CACHE = {}


def kernel(**inp):
    x = np.asarray(inp["x"], dtype=np.float32)
    p = np.asarray(inp["p"], dtype=np.float32)[0]
    B, T, _ = x.shape
    shared = host_prep(inp)
    if "nc" not in _CACHE:
        _CACHE["nc"] = build(4, True)[0]
    nc = _CACHE["nc"]
    in_maps = []
    for c in range(8):
        b, half = c // 2, c % 2
        s0 = half * 2048
        xcore = np.zeros((2560, D), np.float32)
        if half == 1:
            xcore[0:512] = x[b, s0 - 512:s0]
        xcore[512:] = x[b, s0:s0 + 2048]
        m = dict(shared)
        m["xc"] = xcore
        m["pc"] = np.ascontiguousarray(p[b, s0:s0 + 2048])
        cv2 = shared["cvec"].copy()
        cv2[:, C_HM] = 0.0 if half == 1 else NEG
        m["cvec"] = cv2
        in_maps.append(m)
    res = run_bass_kernel_spmd(nc, in_maps, core_ids=list(range(8)))
    out = np.zeros((B, T, D), np.float32)
    for c in range(8):
        b, half = c // 2, c % 2
        out[b, half * 2048:(half + 1) * 2048] = res.results[c]["y"]
    return out
```

```python
import numpy as np
import concourse.bass as bass
import concourse.mybir as mybir
from concourse.bass_utils import run_bass_kernel_spmd

F32 = mybir.dt.float32
BF16 = mybir.dt.bfloat16
AF = mybir.ActivationFunctionType
ALU = mybir.AluOpType
AX = mybir.AxisListType

D = 4096
AW = 2048
NH = 16
TT = 512
EPS = 1e-6
NEG = -1e30
NSLOT = 6
NKS = 4
NCV = 160 + 16 * 31 + 1
C_HM = 160 + 16 * 31
C_NING, C_PNG, C_BDW, C_LNG, C_LNB, C_BPW, C_AOG, C_COG, C_WDW = 0, 32, 64, 80, 96, 112, 128, 144, 160
ENG = ("PE", "ACT", "DVE", "POOL", "SP")
import os
OPT_XH = os.environ.get('KOPT_XH', '1') == '1'
OPT_PIPE = os.environ.get('KOPT_PIPE', '1') == '1'
OPT_P0ACT = os.environ.get('KOPT_P0ACT', '0') == '1'
OPT_P5 = os.environ.get('KOPT_P5', '1') == '1'
OPT_FILL = os.environ.get('KOPT_FILL', '1') == '1'


class Res:
    __slots__ = ("name", "w", "r")

    def __init__(self, name):
        self.name = name
        self.w = None
        self.r = {}


class Sched:
    def __init__(self):
        self.prog = {e: [] for e in ENG}
        self.cnt = {e: 0 for e in ENG}
        self.waited = {e: {} for e in ENG}
        self.dcnt = {}
        self.nins = 0

    def _wait(self, eng, dep):
        key, val, tag = dep
        if tag == eng and eng == "PE":
            return
        if self.waited[eng].get(key, 0) >= val:
            return
        self.prog[eng].append(("wait", key, val))
        self.waited[eng][key] = val

    def _deps(self, eng, reads, writes):
        for r in reads:
            if r.w is not None:
                self._wait(eng, r.w)
        for w in writes:
            if w.w is not None:
                self._wait(eng, w.w)
            for k, (v, t) in w.r.items():
                self._wait(eng, (k, v, t))

    def op(self, eng, fn, reads=(), writes=(), inc=True):
        self._deps(eng, reads, writes)
        if inc:
            self.cnt[eng] += 1
            val = self.cnt[eng]
            self.prog[eng].append(("op", fn, eng, 1))
        else:
            val = self.cnt[eng] + 1
            self.prog[eng].append(("op", fn, None, 0))
        self.nins += 1
        for r in reads:
            r.r[eng] = (val, eng)
        for w in writes:
            w.w = (eng, val, eng)
            w.r = {}

    def dma(self, q, fn, sem, reads=(), writes=()):
        prev = self.dcnt.get(sem, 0)
        if prev:
            self._wait(q, (sem, prev, "dma"))
        self._deps(q, reads, writes)
        val = prev + 16
        self.dcnt[sem] = val
        self.prog[q].append(("op", fn, sem, 16))
        self.nins += 1
        for r in reads:
            r.r[sem] = (val, "dma")
        for w in writes:
            w.w = (sem, val, "dma")
            w.r = {}

    def final_wait_all(self, eng):
        for sem, val in self.dcnt.items():
            self._wait(eng, (sem, val, "dma"))


class Pool:
    def __init__(self, bufs):
        self.bufs = bufs
        self.i = 0

    def get(self):
        b = self.bufs[self.i % len(self.bufs)]
        self.i += 1
        return b


def build(NPASS=4, HALO=True):
    nc = bass.Bass("TRN2", target_bir_lowering=False)
    NTX = (TT if HALO else 0) + TT * NPASS
    NTY = TT * NPASS
    dr = lambda n, s: nc.dram_tensor(n, s, F32, kind="ExternalInput").ap()
    xc = dr("xc", [NTX, D])
    pc = dr("pc", [NTY, 256])
    w_in = dr("w_in", [D, 14336])
    w_pw = dr("w_pw", [AW, AW])
    w_out = dr("w_out", [D, D])
    w_pg = dr("w_pg", [D, D])
    w_ple = dr("w_ple", [256, D])
    cvec_d = dr("cvec", [128, NCV])
    bpg_d = dr("bpg", [1, D])
    fg_d = dr("fg", [1, D])
    bias_d = dr("bias", [NH, 128, 640])
    ident_d = dr("ident", [128, 128])
    wdg = dr("wdg", [16 * 128, 4096])
    y = nc.dram_tensor("y", [NTY, D], F32, kind="ExternalOutput").ap()

    S = Sched()
    sb = lambda n, s, dt=F32: nc.alloc_sbuf_tensor(n, s, dt)
    kts = sb("kts", [128, 20, 512], BF16)
    vts = sb("vts", [128, 20, 512], BF16)
    wb = sb("wb", [128, NSLOT, NKS * 512], BF16)
    yT = sb("yT", [128, 32, 512], BF16)
    CF = sb("CF", [128, 16384], F32)
    cvec = sb("cvec_sb", [128, NCV], F32)
    ident = sb("ident_sb", [128, 128], F32)
    identb = sb("identb", [128, 128], BF16)
    onesm = sb("onesm", [128, 128], F32)
    onesc = sb("onesc", [128, 1], F32)
    uhist = sb("uhist", [128, 16, 30], F32)
    f4 = sb("f4", [128, 4, 544], F32)
    xh = sb("xh", [128, 32, 32], BF16)
    ub = sb("ub", [128, 4, 544], BF16)
    qT = sb("qT", [128, 4, 512], BF16)
    small = sb("small", [128, 64], F32)
    NF = 4
    lbufs = [sb(f"lb{i}", [128, 640], F32) for i in range(2)]
    fbufs = [sb(f"fb{i}", [128, 640], F32) for i in range(NF)]
    NB = 4
    bbufs = [sb(f"bb{i}", [128, 640], BF16) for i in range(NB)]
    banks = [nc.alloc_psum_tensor(f"bank{i}", [128, 512], F32) for i in range(8)]

    CFb = CF[:, 0:8192].bitcast(BF16)
    xnT = CFb.rearrange("p (k t) -> p k t", t=512)
    cv = CF[:, 8192:16384].rearrange("p (c t) -> p c t", t=512)
    hview = CF[:, :].rearrange("p (a f) -> p a f", f=4096)
    xt_bufs = [CF[:, 8192:12288], CF[:, 12288:16384]]

    R_B = [Res(f"B{i}") for i in range(64)]
    R_xn = lambda kc: [R_B[kc]]
    R_cv = lambda cc: [R_B[32 + 2 * cc], R_B[33 + 2 * cc]]
    R_h = lambda tt, cb: [R_B[tt * 16 + cb * 2], R_B[tt * 16 + cb * 2 + 1]]
    R_hrow = lambda tt: R_B[tt * 16:(tt + 1) * 16]
    R_xt = lambda i: R_B[32 + 16 * i:48 + 16 * i]
    R_kt = [Res(f"kt{i}") for i in range(20)]
    R_vt = [Res(f"vt{i}") for i in range(20)]
    R_wb = [Res(f"wb{i}") for i in range(NSLOT)]
    R_yT = [Res(f"yT{i}") for i in range(32)]
    R_bank = [Res(f"bank{i}") for i in range(8)]
    R_cvec, R_ident, R_identb, R_ones = Res("cvec"), Res("ident"), Res("identb"), Res("ones")
    R_uhist = [Res(f"uh{i}") for i in range(16)]
    R_f4 = [Res(f"f4{i}") for i in range(4)]
    R_xh = Res("xh")
    R_ub = [Res(f"ub{i}") for i in range(4)]
    junk, R_junk = f4[:, 0, 0:512], R_f4[0]
    R_qT = [Res(f"qT{i}") for i in range(4)]
    R_small = {}
    FP = Pool([(fbufs[i], Res(f"fb{i}")) for i in range(NF)])
    BP = Pool([(bbufs[i], Res(f"bb{i}")) for i in range(NB)])
    LP = Pool([(lbufs[i], Res(f"lb{i}")) for i in range(2)])

    def sm(name, idx, n=1):
        if name not in R_small:
            R_small[name] = Res(name)
        return small[:, idx:idx + n], R_small[name]

    cvc = lambda c: cvec[:, c:c + 1]

    def ACT(out, in_, func, reads, writes, bias=None, scale=None, accum=None):
        kw = {}
        if bias is not None:
            kw["bias"] = bias
        if scale is not None:
            kw["scale"] = scale
        if accum is not None:
            kw["accum_out"] = accum
        S.op("ACT", lambda e: e.activation(out=out, in_=in_, func=func, **kw), reads, writes)

    def TS(out, in0, s1, op0, reads, writes, s2=None, op1=None, eng="DVE"):
        if op1 is None:
            S.op(eng, lambda e: e.tensor_scalar(out=out, in0=in0, scalar1=s1, scalar2=None, op0=op0), reads, writes)
        else:
            S.op(eng, lambda e: e.tensor_scalar(out=out, in0=in0, scalar1=s1, scalar2=s2, op0=op0, op1=op1), reads, writes)

    def STT(out, in0, sc, in1, op0, op1, reads, writes):
        S.op("DVE", lambda e: e.scalar_tensor_tensor(out=out, in0=in0, scalar=sc, in1=in1, op0=op0, op1=op1), reads, writes)

    def TTo(out, in0, in1, op, reads, writes, eng="DVE"):
        S.op(eng, lambda e: e.tensor_tensor(out=out, in0=in0, in1=in1, op=op), reads, writes)

    def CP(eng, out, in_, reads, writes):
        if eng == "ACT":
            S.op("ACT", lambda e: e.copy(out=out, in_=in_), reads, writes)
        else:
            S.op(eng, lambda e: e.tensor_copy(out=out, in_=in_), reads, writes)

    def MM(out, lhsT, rhs, start, stop, reads, writes, inc=True):
        S.op("PE", lambda e: e.matmul(out, lhsT, rhs, start=start, stop=stop), reads, writes, inc=inc)

    def TR(out, in_, idn, reads, writes, inc=True):
        S.op("PE", lambda e: e.transpose(out, in_, idn), reads, writes, inc=inc)

    def DMA(q, out, in_, sem, reads, writes):
        S.dma(q, lambda e: e.dma_start(out=out, in_=in_), sem, reads, writes)

    def rstd_from(ssq_ap, out_ap, n, reads, writes, inv):
        TS(out_ap, ssq_ap, inv, ALU.mult, reads, writes, s2=EPS, op1=ALU.add)
        ACT(out_ap, out_ap, AF.Sqrt, writes, writes)
        S.op("DVE", lambda e: e.reciprocal(out=out_ap, in_=out_ap), writes, writes)

    wstate = {"i": 0}

    def wload(src, nk=NKS):
        i = wstate["i"] % NSLOT
        wstate["i"] += 1
        dst = wb[:, i, 0:nk * 512].rearrange("p (k c) -> p k c", c=512)
        DMA("POOL", dst, src.rearrange("(k p) c -> p k c", p=128), f"w{i}", [], [R_wb[i]])
        return dst, R_wb[i]

    def wload_flat(src):
        i = wstate["i"] % NSLOT
        wstate["i"] += 1
        dst = wb[:, i, 0:2048]
        DMA("POOL", dst, src, f"w{i}", [], [R_wb[i]])
        return dst, R_wb[i]

    def proj_group(wmat, c0, nkg, rhs_fn, rhs_res_fn, bset=(0, 1, 2, 3), extra_bank=None):
        for _ in proj_group_steps(wmat, c0, nkg, rhs_fn, rhs_res_fn, bset, extra_bank):
            pass

    def proj_group_steps(wmat, c0, nkg, rhs_fn, rhs_res_fn, bset=(0, 1, 2, 3), extra_bank=None):
        for g in range(nkg):
            slot, rs = wload(wmat[g * 512:(g + 1) * 512, c0:c0 + 512])
            for kc in range(NKS):
                k = g * NKS + kc
                for j in range(4):
                    last = (g == nkg - 1 and kc == NKS - 1)
                    if extra_bank is not None:
                        MM(banks[extra_bank][:, j * 32:(j + 1) * 32], slot[:, kc, j * 128:(j + 1) * 128], xh[:, k, :],
                           start=(k == 0 and j == 0), stop=(last and j == 3), reads=[rs, R_xh],
                           writes=[R_bank[extra_bank]], inc=False)
                    MM(banks[bset[j]][:, :], slot[:, kc, j * 128:(j + 1) * 128], rhs_fn(k),
                       start=(k == 0), stop=last, reads=[rs] + rhs_res_fn(k), writes=[R_bank[bset[j]]],
                       inc=(last or (kc == NKS - 1 and j == 3)))
                yield

    xn_rhs = lambda k: xnT[:, k, :]

    DMA("SP", cvec[:, :], cvec_d[:, :], "c0", [], [R_cvec])
    DMA("SP", ident[:, :], ident_d[:, :], "c1", [], [R_ident])
    S.op("DVE", lambda e: e.memset(onesm[:, :], 1.0 / 2048.0), [], [R_ones])
    S.op("DVE", lambda e: e.memset(onesc[:, :], 1.0), [], [R_ones])
    S.op("DVE", lambda e: e.tensor_copy(out=identb[:, :], in_=ident[:, :]), [R_ident], [R_identb])
    S.op("DVE", lambda e: e.memset(uhist[:, :, :], 0.0), [], R_uhist)
    for i in range(20):
        S.op("POOL", lambda e, i=i: e.memset(kts[:, i, :], 0.0), [], [R_kt[i]])
        S.op("POOL", lambda e, i=i: e.memset(vts[:, i, :], 0.0), [], [R_vt[i]])

    hist = list(range(16))
    free = [16, 17, 18, 19]
    xtc = {"i": 0}

    def rms_rows(src_fn, src_res, dst_small, inv):
        ss8 = small[:, 0:8]
        r8 = [sm(f"ss8_{c}", c, 1)[1] for c in range(8)]
        for c in range(8):
            ACT(f4[:, c % 4, 0:512], src_fn(c), AF.Square, src_res, [R_f4[c % 4], r8[c]], accum=ss8[:, c:c + 1])
        tot, rt = sm("sstot", 8, 1)
        S.op("DVE", lambda e: e.tensor_reduce(out=tot, in_=ss8, axis=AX.X, op=ALU.add), r8, [rt])
        rstd_from(tot, dst_small[0], 1, [rt], [dst_small[1]], inv)

    def phase0(t0):
        for tt in range(4):
            bi = xtc["i"] % 2
            xtc["i"] += 1
            xt = xt_bufs[bi]
            rx = R_xt(bi)
            DMA("SP", xt, xc[t0 + tt * 128:t0 + (tt + 1) * 128, :], f"x{bi}", [], rx)
            rs = sm("rstdx", 9, 1)
            rms_rows(lambda c: xt[:, c * 512:(c + 1) * 512], rx, rs, 1.0 / D)
            TS(xt, xt, rs[0], ALU.mult, rx + [rs[1]], rx)
            for k4 in range(8):
                bk = 4 + (k4 % 2)
                for j in range(4):
                    kc = k4 * 4 + j
                    TR(banks[bk][:, j * 128:(j + 1) * 128], xt[:, kc * 128:(kc + 1) * 128], ident[:, :],
                       rx + [R_ident], [R_bank[bk]], inc=(j == 3))
                for j in range(4):
                    kc = k4 * 4 + j
                    if j % 2 == 0 or not OPT_P0ACT:
                        TS(xnT[:, kc, tt * 128:(tt + 1) * 128], banks[bk][:, j * 128:(j + 1) * 128], cvc(C_NING + kc),
                           ALU.mult, [R_bank[bk], R_cvec], R_xn(kc))
                    else:
                        ACT(xnT[:, kc, tt * 128:(tt + 1) * 128], banks[bk][:, j * 128:(j + 1) * 128], AF.Copy,
                            [R_bank[bk], R_cvec], R_xn(kc), scale=cvc(C_NING + kc))

    def conv_phase(first):
        for cgp in range(4):
            proj_group(w_in, 10240 + cgp * 512, 8, xn_rhs, R_xn, extra_bank=(6 if first else None))
            for j in range(4):
                ACT(f4[:, j, 0:512], banks[j][:, :], AF.Sigmoid, [R_bank[j]], [R_f4[j]])
                if first:
                    ACT(f4[:, j, 512:542], banks[6][:, j * 32 + 2:j * 32 + 32], AF.Sigmoid, [R_bank[6]], [R_f4[j]])
            proj_group(w_in, 8192 + cgp * 512, 8, xn_rhs, R_xn, extra_bank=(7 if first else None))
            for j in range(4):
                cc = cgp * 4 + j
                TTo(ub[:, j, 30:542], banks[j][:, :], f4[:, j, 0:512], ALU.mult, [R_bank[j], R_f4[j]], [R_ub[j]])
                if first:
                    TTo(ub[:, j, 0:30], banks[7][:, j * 32 + 2:j * 32 + 32], f4[:, j, 512:542], ALU.mult,
                        [R_bank[7], R_f4[j]], [R_ub[j]])
                else:
                    CP("ACT", ub[:, j, 0:30], uhist[:, cc, :], [R_uhist[cc]], [R_ub[j]])
                CP("ACT", uhist[:, cc, :], ub[:, j, 512:542], [R_ub[j]], [R_uhist[cc]])
            for j in range(4):
                cc = cgp * 4 + j
                rc = R_cv(cc)
                for half in range(2):
                    wsl, rws = wload_flat(wdg[cc * 128:(cc + 1) * 128, half * 2048:(half + 1) * 2048])
                    ntap = 16 if half == 0 else 15
                    for t in range(ntap):
                        tap = half * 16 + t
                        MM(banks[j][:, :], wsl[:, t * 128:(t + 1) * 128], ub[:, j, tap:tap + 512],
                           start=(tap == 0), stop=(tap == 30), reads=[rws, R_ub[j]], writes=[R_bank[j]],
                           inc=(t == ntap - 1))
                ACT(cv[:, cc, :], banks[j][:, :], AF.Identity, [R_bank[j], R_cvec], rc, bias=cvc(C_BDW + cc))
                sq, rsq = FP.get()
                ACT(sq[:, 0:512], cv[:, cc, :], AF.Square, rc, [rsq])
                MM(banks[4][:, :], onesm[:, :], cv[:, cc, :], start=(cc == 0), stop=(cc == 15),
                   reads=[R_ones] + rc, writes=[R_bank[4]])
                MM(banks[5][:, :], onesm[:, :], sq[:, 0:512], start=(cc == 0), stop=(cc == 15),
                   reads=[R_ones, rsq], writes=[R_bank[5]])
        mean, rm = LP.get()
        rstd, rr = LP.get()
        CP("DVE", mean[:, 0:512], banks[4][:, :], [R_bank[4]], [rm])
        TTo(rstd[:, 0:512], mean[:, 0:512], mean[:, 0:512], ALU.mult, [rm], [rr])
        TTo(rstd[:, 0:512], banks[5][:, :], rstd[:, 0:512], ALU.subtract, [R_bank[5], rr], [rr])
        TS(rstd[:, 0:512], rstd[:, 0:512], EPS, ALU.add, [rr], [rr])
        ACT(rstd[:, 0:512], rstd[:, 0:512], AF.Sqrt, [rr], [rr])
        S.op("DVE", lambda e: e.reciprocal(out=rstd[:, 0:512], in_=rstd[:, 0:512]), [rr], [rr])
        for cc in range(16):
            t, rt = FP.get()
            TTo(t[:, 0:512], cv[:, cc, :], mean[:, 0:512], ALU.subtract, R_cv(cc) + [rm], [rt])
            TTo(t[:, 0:512], t[:, 0:512], rstd[:, 0:512], ALU.mult, [rt, rr], [rt])
            ACT(yT[:, cc, :], t[:, 0:512], AF.Silu, [rt, R_cvec], [R_yT[cc]], bias=cvc(C_LNB + cc), scale=cvc(C_LNG + cc))

    def ssq_cols(src_ap, src_res, acc, racc):
        sq, rsq = FP.get()
        ACT(sq[:, 0:512], src_ap, AF.Square, src_res, [rsq])
        for tt in range(4):
            MM(banks[6][:, 320 + tt:321 + tt], sq[:, tt * 128:(tt + 1) * 128], onesc[:, :], start=True, stop=True,
               reads=[rsq, R_ones], writes=[R_bank[6]], inc=(tt == 3))
        TTo(acc, acc, banks[6][:, 320:324], ALU.add, [racc, R_bank[6]], [racc])

    def pointwise_phase():
        acc, racc = sm("ssqc", 16, 4)
        S.op("DVE", lambda e: e.memset(acc, 0.0), [], [racc])
        sT_rhs = lambda k: yT[:, k, :]
        sT_res = lambda k: [R_yT[k]]
        for og in range(4):
            proj_group(w_pw, og * 512, 4, sT_rhs, sT_res)
            for j in range(4):
                oc = og * 4 + j
                ACT(f4[:, j, 0:512], banks[j][:, :], AF.Identity, [R_bank[j], R_cvec], [R_f4[j]], bias=cvc(C_BPW + oc))
                ssq_cols(f4[:, j, 0:512], [R_f4[j]], acc, racc)
            proj_group(w_in, 12288 + og * 512, 8, xn_rhs, R_xn)
            for j in range(4):
                oc = og * 4 + j
                sz, rz = FP.get()
                ACT(sz[:, 0:512], banks[j][:, :], AF.Silu, [R_bank[j]], [rz])
                STT(yT[:, 16 + oc, :], f4[:, j, 0:512], cvc(C_COG + oc), sz[:, 0:512], ALU.mult, ALU.mult,
                    [R_f4[j], R_cvec, rz], [R_yT[16 + oc]])
        rc, rrc = sm("rstdc", 20, 4)
        rstd_from(acc, rc, 4, [racc], [rrc], 1.0 / AW)
        return rc, rrc

    def kv_group(hg, newslots):
        proj_group(w_in, 2048 + hg * 512, 8, xn_rhs, R_xn)
        for j in range(4):
            CP("ACT" if j % 2 else "DVE", kts[:, newslots[j], :], banks[j][:, :], [R_bank[j]], [R_kt[newslots[j]]])
        proj_group(w_in, 4096 + hg * 512, 8, xn_rhs, R_xn)
        for j in range(4):
            vt, rv = BP.get()
            CP("ACT" if j % 2 else "DVE", vt[:, 0:512], banks[j][:, :], [R_bank[j]], [rv])
            b6 = banks[6][:, 0:256].bitcast(BF16)
            for blk in range(4):
                TR(b6[:, blk * 128:(blk + 1) * 128], vt[:, blk * 128:(blk + 1) * 128], identb[:, :],
                   [rv, R_identb], [R_bank[6]], inc=(blk == 3))
            CP("ACT" if j % 2 == 0 else "DVE", vts[:, newslots[j], :], b6, [R_bank[6]], [R_vt[newslots[j]]])

    def attn_phase(first_pass):
        acc, racc = sm("ssqa", 24, 4)
        S.op("DVE", lambda e: e.memset(acc, 0.0), [], [racc])
        pending_q = None
        for hg in range(4):
            newslots = [free.pop(0) for _ in range(4)]
            if pending_q is None:
                proj_group(w_in, hg * 512, 8, xn_rhs, R_xn)
            else:
                for _ in pending_q:
                    pass
            for j in range(4):
                ACT(qT[:, j, :], banks[j][:, :], AF.Copy, [R_bank[j]], [R_qT[j]], scale=float(128 ** -0.5))
            kv_group(hg, newslots)
            proj_group(w_in, 6144 + hg * 512, 8, xn_rhs, R_xn)
            for j in range(4):
                ACT(f4[:, j, 0:512], banks[j][:, :], AF.Silu, [R_bank[j]], [R_f4[j]])
            state = {}
            fillgen = proj_group_steps(w_in, (hg + 1) * 512, 8, xn_rhs, R_xn) if (hg < 3 and OPT_FILL) else None

            def fill(n=1):
                if fillgen is not None:
                    for _ in range(n):
                        next(fillgen, None)

            def stageA(j, pi):
                h = hg * 4 + j
                hs, ns = hist[h], newslots[j]
                if pi == 0:
                    bias, rb = LP.get()
                    DMA("SP", bias[:, :], bias_d[h, :, :], f"bias{LP.i % 2}", [], [rb])
                    state[("bias", j)] = (bias, rb)
                bias, rb = state[("bias", j)]
                nh = 512 - 128 * pi
                ncur = 128 * (pi + 1)
                lq = qT[:, j, pi * 128:(pi + 1) * 128]
                MM(banks[4][:, 0:nh], lq, kts[:, hs, 128 * pi:512], True, True, [R_qT[j], R_kt[hs]], [R_bank[4]])
                MM(banks[5][:, 0:ncur], lq, kts[:, ns, 0:ncur], True, True, [R_qT[j], R_kt[ns]], [R_bank[5]])
                ssb, rss = FP.get()
                if first_pass:
                    STT(ssb[:, 0:nh], banks[4][:, 0:nh], cvc(C_HM), bias[:, 0:nh], ALU.add, ALU.add,
                        [R_bank[4], rb, R_cvec], [rss])
                else:
                    TTo(ssb[:, 0:nh], banks[4][:, 0:nh], bias[:, 0:nh], ALU.add, [R_bank[4], rb], [rss])
                TTo(ssb[:, nh:640], banks[5][:, 0:ncur], bias[:, nh:640], ALU.add, [R_bank[5], rb], [rss])
                u = (j * 4 + pi) % 2
                mx, rmx = sm(f"mx{u}", 28 + u, 1)
                S.op("DVE", lambda e, ssb=ssb, mx=mx: e.tensor_reduce(out=mx, in_=ssb[:, :], axis=AX.X, op=ALU.max,
                                                                     negate=True), [rss], [rmx])
                ex, rex = FP.get()
                rsum, rrs = sm(f"rsum{u}", 30 + u, 1)
                ACT(ex[:, :], ssb[:, :], AF.Exp, [rss, rmx], [rex, rrs], bias=mx, accum=rsum)
                state[("ex", j, pi)] = (ex, rex, rsum, rrs)

            def stageA2(j, pi):
                ex, rex, rsum, rrs = state.pop(("ex", j, pi))
                S.op("DVE", lambda e, rsum=rsum: e.reciprocal(out=rsum, in_=rsum), [rrs], [rrs])
                pn, rpn = BP.get()
                TS(pn[:, :], ex[:, :], rsum, ALU.mult, [rex, rrs], [rpn])
                state[("pn", j, pi)] = (pn, rpn)

            def stageB(j, pi):
                h = hg * 4 + j
                hs, ns = hist[h], newslots[j]
                pn, rpn = state.pop(("pn", j, pi))
                b6 = banks[6][:, 0:320].bitcast(BF16)
                for b in range(5):
                    TR(b6[:, b * 128:(b + 1) * 128], pn[:, b * 128:(b + 1) * 128], identb[:, :],
                       [rpn, R_identb], [R_bank[6]], inc=(b == 4))
                pt, rpt = BP.get()
                CP("ACT", pt[:, :], b6, [R_bank[6]], [rpt])
                fill(1)
                for b in range(5):
                    kb = pi + b
                    if kb < 4:
                        vblk, rvv = vts[:, hs, kb * 128:(kb + 1) * 128], R_vt[hs]
                    else:
                        vblk, rvv = vts[:, ns, (kb - 4) * 128:(kb - 3) * 128], R_vt[ns]
                    MM(banks[7][:, pi * 128:(pi + 1) * 128], vblk, pt[:, b * 128:(b + 1) * 128],
                       start=(b == 0), stop=(b == 4), reads=[rvv, rpt], writes=[R_bank[7]], inc=(b == 4))
                if pi == 3:
                    ssq_cols(banks[7][:, :], [R_bank[7]], acc, racc)
                    STT(yT[:, h, :], banks[7][:, :], cvc(C_AOG + h), f4[:, j, 0:512], ALU.mult, ALU.mult,
                        [R_bank[7], R_cvec, R_f4[j]], [R_yT[h]])

            units = [(j, pi) for j in range(4) for pi in range(4)]
            if OPT_PIPE:
                for ui, (j, pi) in enumerate(units):
                    stageA(j, pi)
                    fill(1)
                    if ui > 0:
                        stageA2(*units[ui - 1])
                        stageB(*units[ui - 1])
                stageA2(*units[-1])
                stageB(*units[-1])
            else:
                for (j, pi) in units:
                    stageA(j, pi)
                    stageA2(j, pi)
                    stageB(j, pi)
            for j in range(4):
                h = hg * 4 + j
                free.append(hist[h])
                hist[h] = newslots[j]
            pending_q = fillgen
        ra, rra = sm("rstda", 32, 4)
        rstd_from(acc, ra, 4, [racc], [rra], 1.0 / AW)
        return ra, rra

    def halo_kv():
        for hg in range(4):
            newslots = [hist[hg * 4 + j] for j in range(4)]
            kv_group(hg, newslots)

    xrc = {"i": 0}
    outc = {"i": 0}

    def out_phase(t0x, t0y, ra, rra, rc, rrc):
        for cb in range(8):
            for tt in range(4):
                si = xrc["i"] % 8
                xrc["i"] += 1
                DMA("SP", hview[:, tt, cb * 512:(cb + 1) * 512],
                    xc[t0x + tt * 128:t0x + (tt + 1) * 128, cb * 512:(cb + 1) * 512], f"xr{si}", [], R_h(tt, cb))
            for g in range(8):
                slot, rs = wload(w_out[g * 512:(g + 1) * 512, cb * 512:(cb + 1) * 512])
                for kc in range(NKS):
                    k = g * NKS + kc
                    for tt in range(4):
                        bk = (0 if g < 4 else 4) + tt
                        last = k in (15, 31)
                        MM(banks[bk][:, :], yT[:, k, tt * 128:(tt + 1) * 128], slot[:, kc, :], start=(k in (0, 16)),
                           stop=last, reads=[rs, R_yT[k]], writes=[R_bank[bk]], inc=(last or (kc == NKS - 1 and tt == 3)))
                if g == 3:
                    for tt in range(4):
                        hv = hview[:, tt, cb * 512:(cb + 1) * 512]
                        STT(hv, banks[tt][:, :], ra[:, tt:tt + 1], hv, ALU.mult, ALU.add,
                            [R_bank[tt], rra] + R_h(tt, cb), R_h(tt, cb))
                if g == 7:
                    for tt in range(4):
                        hv = hview[:, tt, cb * 512:(cb + 1) * 512]
                        STT(hv, banks[4 + tt][:, :], rc[:, tt:tt + 1], hv, ALU.mult, ALU.add,
                            [R_bank[4 + tt], rrc] + R_h(tt, cb), R_h(tt, cb))
        rh, rrh = sm("rstdh", 36, 4)
        for tt in range(4):
            rms_rows(lambda c, tt=tt: hview[:, tt, c * 512:(c + 1) * 512], R_hrow(tt), (rh[:, tt:tt + 1], rrh), 1.0 / D)
        for tt in range(4):
            for k4 in range(8):
                bk = 6 + (k4 % 2)
                for j in range(4):
                    kc = k4 * 4 + j
                    TR(banks[bk][:, j * 128:(j + 1) * 128], hview[:, tt, kc * 128:(kc + 1) * 128], ident[:, :],
                       R_h(tt, kc // 4) + [R_ident], [R_bank[bk]], inc=(j == 3))
                for j in range(4):
                    kc = k4 * 4 + j
                    TS(yT[:, kc, tt * 128:(tt + 1) * 128], banks[bk][:, j * 128:(j + 1) * 128], cvc(C_PNG + kc),
                       ALU.mult, [R_bank[bk], R_cvec], [R_yT[kc]])
        for tt in range(4):
            ptile, rp = FP.get()
            DMA("SP", ptile[:, 0:256], pc[t0y + tt * 128:t0y + (tt + 1) * 128, :], f"p{tt % 2}", [], [rp])
            for k2 in range(2):
                TR(banks[5][:, k2 * 128:(k2 + 1) * 128], ptile[:, k2 * 128:(k2 + 1) * 128], ident[:, :],
                   [rp, R_ident], [R_bank[5]], inc=(k2 == 1))
            for k2 in range(2):
                CP("ACT", qT[:, k2, tt * 128:(tt + 1) * 128], banks[5][:, k2 * 128:(k2 + 1) * 128], [R_bank[5]], [R_qT[k2]])
        for cb in range(8):
            bbc, rbb = LP.get()
            DMA("SP", bbc[:, 0:512], bpg_d[0:1, cb * 512:(cb + 1) * 512].partition_broadcast(128), f"bb{cb % 2}", [], [rbb])
            for g in range(8):
                slot, rs = wload(w_pg[g * 512:(g + 1) * 512, cb * 512:(cb + 1) * 512])
                for kc in range(NKS):
                    k = g * NKS + kc
                    for tt in range(4):
                        last = (k == 31)
                        MM(banks[tt][:, :], yT[:, k, tt * 128:(tt + 1) * 128], slot[:, kc, :], start=(k == 0), stop=last,
                           reads=[rs, R_yT[k]], writes=[R_bank[tt]], inc=(last or (kc == NKS - 1 and tt == 3)))
            pslot, rps = wload(w_ple[:, cb * 512:(cb + 1) * 512], nk=2)
            for tt in range(4):
                for k2 in range(2):
                    MM(banks[4 + tt][:, :], qT[:, k2, tt * 128:(tt + 1) * 128], pslot[:, k2, :], start=(k2 == 0),
                       stop=(k2 == 1), reads=[rps, R_qT[k2]], writes=[R_bank[4 + tt]], inc=(k2 == 1))
            ts_ = []
            for tt in range(4):
                t, rt = FP.get()
                ts_.append((t, rt))
                STT(t[:, 0:512], banks[tt][:, :], rh[:, tt:tt + 1], bbc[:, 0:512], ALU.mult, ALU.add,
                    [R_bank[tt], rrh, rbb], [rt])
            for tt in range(4):
                t, rt = ts_[tt]
                ACT(t[:, 0:512], t[:, 0:512], AF.Sigmoid, [rt], [rt])
            for tt in range(4):
                t, rt = ts_[tt]
                TTo(t[:, 0:512], t[:, 0:512], banks[4 + tt][:, :], ALU.mult, [rt, R_bank[4 + tt]], [rt])
                hv = hview[:, tt, cb * 512:(cb + 1) * 512]
                TTo(hv, hv, t[:, 0:512], ALU.add, R_h(tt, cb) + [rt], R_h(tt, cb))
        rf, rrf = sm("rstdf", 40, 4)
        for tt in range(4):
            rms_rows(lambda c, tt=tt: hview[:, tt, c * 512:(c + 1) * 512], R_hrow(tt), (rf[:, tt:tt + 1], rrf), 1.0 / D)
        for cb in range(8):
            fgb, rfg = LP.get()
            DMA("SP", fgb[:, 0:512], fg_d[0:1, cb * 512:(cb + 1) * 512].partition_broadcast(128), f"fg{cb % 2}", [], [rfg])
            for tt in range(4):
                o, ro = FP.get()
                STT(o[:, 0:512], hview[:, tt, cb * 512:(cb + 1) * 512], rf[:, tt:tt + 1], fgb[:, 0:512], ALU.mult, ALU.mult,
                    R_h(tt, cb) + [rrf, rfg], [ro])
                si = outc["i"] % 4
                outc["i"] += 1
                DMA("SP", y[t0y + tt * 128:t0y + (tt + 1) * 128, cb * 512:(cb + 1) * 512], o[:, 0:512], f"o{si}", [ro], [])

    if HALO:
        phase0(0)
        for k in range(32):
            CP("ACT" if k % 2 else "DVE", xh[:, k, :], xnT[:, k, 480:512], R_xn(k), [R_xh])
        halo_kv()
    else:
        S.op("DVE", lambda e: e.memset(xh[:, :, :], 0.0), [], [R_xh])
    for pp in range(NPASS):
        t0y = pp * TT
        t0x = t0y + (TT if HALO else 0)
        phase0(t0x)
        conv_phase(pp == 0 and OPT_XH)
        rc, rrc = pointwise_phase()
        ra, rra = attn_phase(pp == 0)
        out_phase(t0x, t0y, ra, rra, rc, rrc)
    S.final_wait_all("SP")

    sem_names = set(ENG) | set(S.dcnt.keys())
    sems = {}
    import contextlib
    with contextlib.ExitStack() as st:
        for n in sorted(sem_names):
            sems[n] = st.enter_context(nc.semaphore("s_" + n))
        block = st.enter_context(nc.Block())

        def emit(eng_obj, name):
            for item in S.prog[name]:
                if item[0] == "wait":
                    eng_obj.wait_ge(sems[item[1]], item[2])
                else:
                    ins = item[1](eng_obj)
                    if item[3]:
                        ins.then_inc(sems[item[2]], item[3])

        @block.tensor
        def _(e):
            emit(e, "PE")

        @block.scalar
        def _(e):
            emit(e, "ACT")

        @block.vector
        def _(e):
            emit(e, "DVE")

        @block.gpsimd
        def _(e):
            emit(e, "POOL")

        @block.sync
        def _(e):
            emit(e, "SP")
    return nc, S


def host_prep(inp):
    f = np.float32
    g = lambda k: np.asarray(inp[k], dtype=f)
    col = lambda v, n: np.ascontiguousarray(v.reshape(n, 128).T)
    cvec = np.zeros((128, NCV), f)
    cvec[:, C_NING:C_NING + 32] = col(g("norm_in_g")[0], 32)
    cvec[:, C_PNG:C_PNG + 32] = col(g("ple_norm_g")[0], 32)
    cvec[:, C_BDW:C_BDW + 16] = col(g("b_dw")[0], 16)
    cvec[:, C_LNG:C_LNG + 16] = col(g("conv_ln_g")[0], 16)
    cvec[:, C_LNB:C_LNB + 16] = col(g("conv_ln_b")[0], 16)
    cvec[:, C_BPW:C_BPW + 16] = col(g("b_pw")[0], 16)
    cvec[:, C_AOG:C_AOG + 16] = col(g("attn_out_g")[0], 16)
    cvec[:, C_COG:C_COG + 16] = col(g("conv_out_g")[0], 16)
    wdw = g("w_dw")[0].reshape(31, 16, 128).transpose(2, 1, 0)
    cvec[:, C_WDW:C_WDW + 496] = wdw.reshape(128, 496)
    q = np.arange(128)[:, None]
    j = np.arange(640)[None, :]
    idx = np.clip(512 + q - j, -256, 256) + 256
    tab = g("rel_table")[0]
    bias = tab[:, idx]
    qc, kc = q // 64, j // 64
    valid = (kc >= qc) & (kc <= 8 + qc)
    bias = np.where(valid[None], bias, f(NEG)).astype(f)
    wd = g("w_dw")[0]
    wdg = np.zeros((16, 128, 32, 128), f)
    ar = np.arange(128)
    for cc in range(16):
        wdg[cc, ar, :31, ar] = wd[:, cc * 128:(cc + 1) * 128].T
    shared = {
        "wdg": wdg.reshape(16 * 128, 4096),
        "w_in": np.ascontiguousarray(g("w_in")[0]), "w_pw": np.ascontiguousarray(g("w_pw")[0]),
        "w_out": np.ascontiguousarray(g("w_out")[0]), "w_pg": np.ascontiguousarray(g("w_ple_gate")[0]),
        "w_ple": np.ascontiguousarray(g("w_ple")[0]), "cvec": cvec,
        "bpg": np.ascontiguousarray(g("b_ple_gate")[0][None, :]), "fg": np.ascontiguousarray(g("final_g")[None, :]),
        "bias": np.ascontiguousarray(bias), "ident": np.eye(128, dtype=f),
    }
    return shared


_CACHE = {}


def kernel(**inp):
    x = np.asarray(inp["x"], dtype=np.float32)
    p = np.asarray(inp["p"], dtype=np.float32)[0]
    B, T, _ = x.shape
    shared = host_prep(inp)
    if "nc" not in _CACHE:
        _CACHE["nc"] = build(4, True)[0]
    nc = _CACHE["nc"]
    in_maps = []
    for c in range(8):
        b, half = c // 2, c % 2
        s0 = half * 2048
        xcore = np.zeros((2560, D), np.float32)
        if half == 1:
            xcore[0:512] = x[b, s0 - 512:s0]
        xcore[512:] = x[b, s0:s0 + 2048]
        m = dict(shared)
        m["xc"] = xcore
        m["pc"] = np.ascontiguousarray(p[b, s0:s0 + 2048])
        cv2 = shared["cvec"].copy()
        cv2[:, C_HM] = 0.0 if half == 1 else NEG
        m["cvec"] = cv2
        in_maps.append(m)
    res = run_bass_kernel_spmd(nc, in_maps, core_ids=list(range(8)))
    out = np.zeros((B, T, D), np.float32)
    for c in range(8):
        b, half = c // 2, c % 2
        out[b, half * 2048:(half + 1) * 2048] = res.results[c]["y"]
    return out
```

```python
import numpy as np
import concourse.bass as bass
import concourse.mybir as mybir
from concourse.bass_utils import run_bass_kernel_spmd

F32 = mybir.dt.float32
BF16 = mybir.dt.bfloat16
AF = mybir.ActivationFunctionType
ALU = mybir.AluOpType
AX = mybir.AxisListType

D = 4096
AW = 2048
NH = 16
TT = 512
EPS = 1e-6
NEG = -1e30
NSLOT = 6
NKS = 4
NCV = 160 + 16 * 31 + 1
C_HM = 160 + 16 * 31
C_NING, C_PNG, C_BDW, C_LNG, C_LNB, C_BPW, C_AOG, C_COG, C_WDW = 0, 32, 64, 80, 96, 112, 128, 144, 160
ENG = ("PE", "ACT", "DVE", "POOL", "SP")
import os
OPT_XH = os.environ.get('KOPT_XH', '1') == '1'
OPT_PIPE = os.environ.get('KOPT_PIPE', '1') == '1'
OPT_P0ACT = os.environ.get('KOPT_P0ACT', '0') == '1'
OPT_P5 = os.environ.get('KOPT_P5', '1') == '1'
OPT_FILL = os.environ.get('KOPT_FILL', '1') == '1'


class Res:
    __slots__ = ("name", "w", "r")

    def __init__(self, name):
        self.name = name
        self.w = None
        self.r = {}


class Sched:
    def __init__(self):
        self.prog = {e: [] for e in ENG}
        self.cnt = {e: 0 for e in ENG}
        self.waited = {e: {} for e in ENG}
        self.dcnt = {}
        self.nins = 0

    def _wait(self, eng, dep):
        key, val, tag = dep
        if tag == eng and eng == "PE":
            return
        if self.waited[eng].get(key, 0) >= val:
            return
        self.prog[eng].append(("wait", key, val))
        self.waited[eng][key] = val

    def _deps(self, eng, reads, writes):
        for r in reads:
            if r.w is not None:
                self._wait(eng, r.w)
        for w in writes:
            if w.w is not None:
                self._wait(eng, w.w)
            for k, (v, t) in w.r.items():
                self._wait(eng, (k, v, t))

    def op(self, eng, fn, reads=(), writes=(), inc=True):
        self._deps(eng, reads, writes)
        if inc:
            self.cnt[eng] += 1
            val = self.cnt[eng]
            self.prog[eng].append(("op", fn, eng, 1))
        else:
            val = self.cnt[eng] + 1
            self.prog[eng].append(("op", fn, None, 0))
        self.nins += 1
        for r in reads:
            r.r[eng] = (val, eng)
        for w in writes:
            w.w = (eng, val, eng)
            w.r = {}

    def dma(self, q, fn, sem, reads=(), writes=()):
        prev = self.dcnt.get(sem, 0)
        if prev:
            self._wait(q, (sem, prev, "dma"))
        self._deps(q, reads, writes)
        val = prev + 16
        self.dcnt[sem] = val
        self.prog[q].append(("op", fn, sem, 16))
        self.nins += 1
        for r in reads:
            r.r[sem] = (val, "dma")
        for w in writes:
            w.w = (sem, val, "dma")
            w.r = {}

    def final_wait_all(self, eng):
        for sem, val in self.dcnt.items():
            self._wait(eng, (sem, val, "dma"))


class Pool:
    def __init__(self, bufs):
        self.bufs = bufs
        self.i = 0

    def get(self):
        b = self.bufs[self.i % len(self.bufs)]
        self.i += 1
        return b


def build(NPASS=4, HALO=True):
    nc = bass.Bass("TRN2", target_bir_lowering=False)
    NTX = (TT if HALO else 0) + TT * NPASS
    NTY = TT * NPASS
    dr = lambda n, s: nc.dram_tensor(n, s, F32, kind="ExternalInput").ap()
    xc = dr("xc", [NTX, D])
    pc = dr("pc", [NTY, 256])
    w_in = dr("w_in", [D, 14336])
    w_pw = dr("w_pw", [AW, AW])
    w_out = dr("w_out", [D, D])
    w_pg = dr("w_pg", [D, D])
    w_ple = dr("w_ple", [256, D])
    cvec_d = dr("cvec", [128, NCV])
    bpg_d = dr("bpg", [1, D])
    fg_d = dr("fg", [1, D])
    bias_d = dr("bias", [NH, 128, 640])
    ident_d = dr("ident", [128, 128])
    wdg = dr("wdg", [16 * 128, 4096])
    y = nc.dram_tensor("y", [NTY, D], F32, kind="ExternalOutput").ap()

    S = Sched()
    sb = lambda n, s, dt=F32: nc.alloc_sbuf_tensor(n, s, dt)
    kts = sb("kts", [128, 20, 512], BF16)
    vts = sb("vts", [128, 20, 512], BF16)
    wb = sb("wb", [128, NSLOT, NKS * 512], BF16)
    yT = sb("yT", [128, 32, 512], BF16)
    CF = sb("CF", [128, 16384], F32)
    cvec = sb("cvec_sb", [128, NCV], F32)
    ident = sb("ident_sb", [128, 128], F32)
    identb = sb("identb", [128, 128], BF16)
    onesm = sb("onesm", [128, 128], F32)
    onesc = sb("onesc", [128, 1], F32)
    uhist = sb("uhist", [128, 16, 30], F32)
    f4 = sb("f4", [128, 4, 544], F32)
    xh = sb("xh", [128, 32, 32], BF16)
    ub = sb("ub", [128, 4, 544], BF16)
    qT = sb("qT", [128, 4, 512], BF16)
    small = sb("small", [128, 64], F32)
    small2 = sb("small2", [128, 64], F32)
    NF = 4
    lbufs = [sb(f"lb{i}", [128, 640], F32) for i in range(2)]
    fbufs = [sb(f"fb{i}", [128, 640], F32) for i in range(NF)]
    NB = 4
    bbufs = [sb(f"bb{i}", [128, 640], BF16) for i in range(NB)]
    banks = [nc.alloc_psum_tensor(f"bank{i}", [128, 512], F32) for i in range(8)]

    CFb = CF[:, 0:8192].bitcast(BF16)
    xnT = CFb.rearrange("p (k t) -> p k t", t=512)
    cv = CF[:, 8192:16384].rearrange("p (c t) -> p c t", t=512)
    hview = CF[:, :].rearrange("p (a f) -> p a f", f=4096)
    xt_bufs = [CF[:, 8192:12288], CF[:, 12288:16384]]

    R_B = [Res(f"B{i}") for i in range(64)]
    R_xn = lambda kc: [R_B[kc]]
    R_cv = lambda cc: [R_B[32 + 2 * cc], R_B[33 + 2 * cc]]
    R_h = lambda tt, cb: [R_B[tt * 16 + cb * 2], R_B[tt * 16 + cb * 2 + 1]]
    R_hrow = lambda tt: R_B[tt * 16:(tt + 1) * 16]
    R_xt = lambda i: R_B[32 + 16 * i:48 + 16 * i]
    R_kt = [Res(f"kt{i}") for i in range(20)]
    R_vt = [Res(f"vt{i}") for i in range(20)]
    R_wb = [Res(f"wb{i}") for i in range(NSLOT)]
    R_yT = [Res(f"yT{i}") for i in range(32)]
    R_bank = [Res(f"bank{i}") for i in range(8)]
    R_cvec, R_ident, R_identb, R_ones = Res("cvec"), Res("ident"), Res("identb"), Res("ones")
    R_uhist = [Res(f"uh{i}") for i in range(16)]
    R_f4 = [Res(f"f4{i}") for i in range(4)]
    R_xh = Res("xh")
    R_ub = [Res(f"ub{i}") for i in range(4)]
    junk, R_junk = f4[:, 0, 0:512], R_f4[0]
    R_qT = [Res(f"qT{i}") for i in range(4)]
    R_small = {}
    FP = Pool([(fbufs[i], Res(f"fb{i}")) for i in range(NF)])
    BP = Pool([(bbufs[i], Res(f"bb{i}")) for i in range(NB)])
    LP = Pool([(lbufs[i], Res(f"lb{i}")) for i in range(2)])

    def sm(name, idx, n=1):
        if name not in R_small:
            R_small[name] = Res(name)
        return small[:, idx:idx + n], R_small[name]

    cvc = lambda c: cvec[:, c:c + 1]

    def ACT(out, in_, func, reads, writes, bias=None, scale=None, accum=None):
        kw = {}
        if bias is not None:
            kw["bias"] = bias
        if scale is not None:
            kw["scale"] = scale
        if accum is not None:
            kw["accum_out"] = accum
        S.op("ACT", lambda e: e.activation(out=out, in_=in_, func=func, **kw), reads, writes)

    def TS(out, in0, s1, op0, reads, writes, s2=None, op1=None, eng="DVE"):
        if op1 is None:
            S.op(eng, lambda e: e.tensor_scalar(out=out, in0=in0, scalar1=s1, scalar2=None, op0=op0), reads, writes)
        else:
            S.op(eng, lambda e: e.tensor_scalar(out=out, in0=in0, scalar1=s1, scalar2=s2, op0=op0, op1=op1), reads, writes)

    def STT(out, in0, sc, in1, op0, op1, reads, writes):
        S.op("DVE", lambda e: e.scalar_tensor_tensor(out=out, in0=in0, scalar=sc, in1=in1, op0=op0, op1=op1), reads, writes)

    def TTo(out, in0, in1, op, reads, writes, eng="DVE"):
        S.op(eng, lambda e: e.tensor_tensor(out=out, in0=in0, in1=in1, op=op), reads, writes)

    def CP(eng, out, in_, reads, writes):
        if eng == "ACT":
            S.op("ACT", lambda e: e.copy(out=out, in_=in_), reads, writes)
        else:
            S.op(eng, lambda e: e.tensor_copy(out=out, in_=in_), reads, writes)

    def MM(out, lhsT, rhs, start, stop, reads, writes, inc=True):
        S.op("PE", lambda e: e.matmul(out, lhsT, rhs, start=start, stop=stop), reads, writes, inc=inc)

    def TR(out, in_, idn, reads, writes, inc=True):
        S.op("PE", lambda e: e.transpose(out, in_, idn), reads, writes, inc=inc)

    def DMA(q, out, in_, sem, reads, writes):
        S.dma(q, lambda e: e.dma_start(out=out, in_=in_), sem, reads, writes)

    def rstd_from(ssq_ap, out_ap, n, reads, writes, inv):
        TS(out_ap, ssq_ap, inv, ALU.mult, reads, writes, s2=EPS, op1=ALU.add)
        ACT(out_ap, out_ap, AF.Sqrt, writes, writes)
        S.op("DVE", lambda e: e.reciprocal(out=out_ap, in_=out_ap), writes, writes)

    wstate = {"i": 0}

    def wload(src, nk=NKS):
        i = wstate["i"] % NSLOT
        wstate["i"] += 1
        dst = wb[:, i, 0:nk * 512].rearrange("p (k c) -> p k c", c=512)
        DMA("POOL", dst, src.rearrange("(k p) c -> p k c", p=128), f"w{i}", [], [R_wb[i]])
        return dst, R_wb[i]

    def wload_flat(src):
        i = wstate["i"] % NSLOT
        wstate["i"] += 1
        dst = wb[:, i, 0:2048]
        DMA("POOL", dst, src, f"w{i}", [], [R_wb[i]])
        return dst, R_wb[i]

    def proj_group(wmat, c0, nkg, rhs_fn, rhs_res_fn, bset=(0, 1, 2, 3), extra_bank=None):
        for _ in proj_group_steps(wmat, c0, nkg, rhs_fn, rhs_res_fn, bset, extra_bank):
            pass

    def proj_group_steps(wmat, c0, nkg, rhs_fn, rhs_res_fn, bset=(0, 1, 2, 3), extra_bank=None):
        for g in range(nkg):
            slot, rs = wload(wmat[g * 512:(g + 1) * 512, c0:c0 + 512])
            for kc in range(NKS):
                k = g * NKS + kc
                for j in range(4):
                    last = (g == nkg - 1 and kc == NKS - 1)
                    if extra_bank is not None:
                        MM(banks[extra_bank][:, j * 32:(j + 1) * 32], slot[:, kc, j * 128:(j + 1) * 128], xh[:, k, :],
                           start=(k == 0 and j == 0), stop=(last and j == 3), reads=[rs, R_xh],
                           writes=[R_bank[extra_bank]], inc=False)
                    MM(banks[bset[j]][:, :], slot[:, kc, j * 128:(j + 1) * 128], rhs_fn(k),
                       start=(k == 0), stop=last, reads=[rs] + rhs_res_fn(k), writes=[R_bank[bset[j]]],
                       inc=(last or (kc == NKS - 1 and j == 3)))
                yield

    xn_rhs = lambda k: xnT[:, k, :]

    DMA("SP", cvec[:, :], cvec_d[:, :], "c0", [], [R_cvec])
    DMA("SP", ident[:, :], ident_d[:, :], "c1", [], [R_ident])
    S.op("DVE", lambda e: e.memset(onesm[:, :], 1.0 / 2048.0), [], [R_ones])
    S.op("DVE", lambda e: e.memset(onesc[:, :], 1.0), [], [R_ones])
    S.op("DVE", lambda e: e.tensor_copy(out=identb[:, :], in_=ident[:, :]), [R_ident], [R_identb])
    S.op("DVE", lambda e: e.memset(uhist[:, :, :], 0.0), [], R_uhist)
    for i in range(20):
        S.op("POOL", lambda e, i=i: e.memset(kts[:, i, :], 0.0), [], [R_kt[i]])
        S.op("POOL", lambda e, i=i: e.memset(vts[:, i, :], 0.0), [], [R_vt[i]])

    hist = list(range(16))
    free = [16, 17, 18, 19]
    xtc = {"i": 0}

    def rms_rows(src_fn, src_res, dst_small, inv):
        ss8 = small[:, 0:8]
        r8 = [sm(f"ss8_{c}", c, 1)[1] for c in range(8)]
        for c in range(8):
            ACT(f4[:, c % 4, 0:512], src_fn(c), AF.Square, src_res, [R_f4[c % 4], r8[c]], accum=ss8[:, c:c + 1])
        tot, rt = sm("sstot", 8, 1)
        S.op("DVE", lambda e: e.tensor_reduce(out=tot, in_=ss8, axis=AX.X, op=ALU.add), r8, [rt])
        rstd_from(tot, dst_small[0], 1, [rt], [dst_small[1]], inv)

    R_s2 = {}

    def ssq_block(base, tt, cb, hv, hres):
        key = (base, tt, cb)
        R_s2[key] = Res(f"s2_{base}_{tt}_{cb}")
        c = base + tt * 8 + cb
        ACT(f4[:, (tt + cb) % 4, 0:512], hv, AF.Square, hres, [R_f4[(tt + cb) % 4], R_s2[key]],
            accum=small2[:, c:c + 1])

    def rstd_rows_from_blocks(base, tt, dst_small, inv):
        tot, rt = sm("sstot", 8, 1)
        src = small2[:, base + tt * 8:base + tt * 8 + 8]
        S.op("DVE", lambda e: e.tensor_reduce(out=tot, in_=src, axis=AX.X, op=ALU.add),
             [R_s2[(base, tt, cb)] for cb in range(8)], [rt])
        rstd_from(tot, dst_small[0], 1, [rt], [dst_small[1]], inv)

    def phase0(t0):
        for tt in range(4):
            bi = xtc["i"] % 2
            xtc["i"] += 1
            xt = xt_bufs[bi]
            rx = R_xt(bi)
            DMA("SP", xt, xc[t0 + tt * 128:t0 + (tt + 1) * 128, :], f"x{bi}", [], rx)
            rs = sm("rstdx", 9, 1)
            rms_rows(lambda c: xt[:, c * 512:(c + 1) * 512], rx, rs, 1.0 / D)
            TS(xt, xt, rs[0], ALU.mult, rx + [rs[1]], rx)
            for k4 in range(8):
                bk = 4 + (k4 % 2)
                for j in range(4):
                    kc = k4 * 4 + j
                    TR(banks[bk][:, j * 128:(j + 1) * 128], xt[:, kc * 128:(kc + 1) * 128], ident[:, :],
                       rx + [R_ident], [R_bank[bk]], inc=(j == 3))
                for j in range(4):
                    kc = k4 * 4 + j
                    if j % 2 == 0 or not OPT_P0ACT:
                        TS(xnT[:, kc, tt * 128:(tt + 1) * 128], banks[bk][:, j * 128:(j + 1) * 128], cvc(C_NING + kc),
                           ALU.mult, [R_bank[bk], R_cvec], R_xn(kc))
                    else:
                        ACT(xnT[:, kc, tt * 128:(tt + 1) * 128], banks[bk][:, j * 128:(j + 1) * 128], AF.Copy,
                            [R_bank[bk], R_cvec], R_xn(kc), scale=cvc(C_NING + kc))

    def conv_phase(first):
        for cgp in range(4):
            proj_group(w_in, 10240 + cgp * 512, 8, xn_rhs, R_xn, extra_bank=(6 if first else None))
            for j in range(4):
                ACT(f4[:, j, 0:512], banks[j][:, :], AF.Sigmoid, [R_bank[j]], [R_f4[j]])
                if first:
                    ACT(f4[:, j, 512:542], banks[6][:, j * 32 + 2:j * 32 + 32], AF.Sigmoid, [R_bank[6]], [R_f4[j]])
            proj_group(w_in, 8192 + cgp * 512, 8, xn_rhs, R_xn, extra_bank=(7 if first else None))
            for j in range(4):
                cc = cgp * 4 + j
                TTo(ub[:, j, 30:542], banks[j][:, :], f4[:, j, 0:512], ALU.mult, [R_bank[j], R_f4[j]], [R_ub[j]])
                if first:
                    TTo(ub[:, j, 0:30], banks[7][:, j * 32 + 2:j * 32 + 32], f4[:, j, 512:542], ALU.mult,
                        [R_bank[7], R_f4[j]], [R_ub[j]])
                else:
                    CP("ACT", ub[:, j, 0:30], uhist[:, cc, :], [R_uhist[cc]], [R_ub[j]])
                CP("ACT", uhist[:, cc, :], ub[:, j, 512:542], [R_ub[j]], [R_uhist[cc]])
            for j in range(4):
                cc = cgp * 4 + j
                rc = R_cv(cc)
                for half in range(2):
                    wsl, rws = wload_flat(wdg[cc * 128:(cc + 1) * 128, half * 2048:(half + 1) * 2048])
                    ntap = 16 if half == 0 else 15
                    for t in range(ntap):
                        tap = half * 16 + t
                        MM(banks[j][:, :], wsl[:, t * 128:(t + 1) * 128], ub[:, j, tap:tap + 512],
                           start=(tap == 0), stop=(tap == 30), reads=[rws, R_ub[j]], writes=[R_bank[j]],
                           inc=(t == ntap - 1))
                ACT(cv[:, cc, :], banks[j][:, :], AF.Identity, [R_bank[j], R_cvec], rc, bias=cvc(C_BDW + cc))
                sq, rsq = FP.get()
                ACT(sq[:, 0:512], cv[:, cc, :], AF.Square, rc, [rsq])
                MM(banks[4][:, :], onesm[:, :], cv[:, cc, :], start=(cc == 0), stop=(cc == 15),
                   reads=[R_ones] + rc, writes=[R_bank[4]])
                MM(banks[5][:, :], onesm[:, :], sq[:, 0:512], start=(cc == 0), stop=(cc == 15),
                   reads=[R_ones, rsq], writes=[R_bank[5]])
        mean, rm = LP.get()
        rstd, rr = LP.get()
        CP("DVE", mean[:, 0:512], banks[4][:, :], [R_bank[4]], [rm])
        TTo(rstd[:, 0:512], mean[:, 0:512], mean[:, 0:512], ALU.mult, [rm], [rr])
        TTo(rstd[:, 0:512], banks[5][:, :], rstd[:, 0:512], ALU.subtract, [R_bank[5], rr], [rr])
        TS(rstd[:, 0:512], rstd[:, 0:512], EPS, ALU.add, [rr], [rr])
        ACT(rstd[:, 0:512], rstd[:, 0:512], AF.Sqrt, [rr], [rr])
        S.op("DVE", lambda e: e.reciprocal(out=rstd[:, 0:512], in_=rstd[:, 0:512]), [rr], [rr])
        for cc in range(16):
            t, rt = FP.get()
            TTo(t[:, 0:512], cv[:, cc, :], mean[:, 0:512], ALU.subtract, R_cv(cc) + [rm], [rt])
            TTo(t[:, 0:512], t[:, 0:512], rstd[:, 0:512], ALU.mult, [rt, rr], [rt])
            ACT(yT[:, cc, :], t[:, 0:512], AF.Silu, [rt, R_cvec], [R_yT[cc]], bias=cvc(C_LNB + cc), scale=cvc(C_LNG + cc))

    def ssq_cols(src_ap, src_res, acc, racc):
        sq, rsq = FP.get()
        ACT(sq[:, 0:512], src_ap, AF.Square, src_res, [rsq])
        for tt in range(4):
            MM(banks[6][:, 320 + tt:321 + tt], sq[:, tt * 128:(tt + 1) * 128], onesc[:, :], start=True, stop=True,
               reads=[rsq, R_ones], writes=[R_bank[6]], inc=(tt == 3))
        TTo(acc, acc, banks[6][:, 320:324], ALU.add, [racc, R_bank[6]], [racc])

    def pointwise_phase():
        acc, racc = sm("ssqc", 16, 4)
        S.op("DVE", lambda e: e.memset(acc, 0.0), [], [racc])
        sT_rhs = lambda k: yT[:, k, :]
        sT_res = lambda k: [R_yT[k]]
        for og in range(4):
            proj_group(w_pw, og * 512, 4, sT_rhs, sT_res)
            for j in range(4):
                oc = og * 4 + j
                ACT(f4[:, j, 0:512], banks[j][:, :], AF.Identity, [R_bank[j], R_cvec], [R_f4[j]], bias=cvc(C_BPW + oc))
                ssq_cols(f4[:, j, 0:512], [R_f4[j]], acc, racc)
            proj_group(w_in, 12288 + og * 512, 8, xn_rhs, R_xn)
            for j in range(4):
                oc = og * 4 + j
                sz, rz = FP.get()
                ACT(sz[:, 0:512], banks[j][:, :], AF.Silu, [R_bank[j]], [rz])
                STT(yT[:, 16 + oc, :], f4[:, j, 0:512], cvc(C_COG + oc), sz[:, 0:512], ALU.mult, ALU.mult,
                    [R_f4[j], R_cvec, rz], [R_yT[16 + oc]])
        rc, rrc = sm("rstdc", 20, 4)
        rstd_from(acc, rc, 4, [racc], [rrc], 1.0 / AW)
        return rc, rrc

    def kv_group(hg, newslots):
        proj_group(w_in, 2048 + hg * 512, 8, xn_rhs, R_xn)
        for j in range(4):
            CP("ACT" if j % 2 else "DVE", kts[:, newslots[j], :], banks[j][:, :], [R_bank[j]], [R_kt[newslots[j]]])
        proj_group(w_in, 4096 + hg * 512, 8, xn_rhs, R_xn)
        for j in range(4):
            vt, rv = BP.get()
            CP("ACT" if j % 2 else "DVE", vt[:, 0:512], banks[j][:, :], [R_bank[j]], [rv])
            b6 = banks[6][:, 0:256].bitcast(BF16)
            for blk in range(4):
                TR(b6[:, blk * 128:(blk + 1) * 128], vt[:, blk * 128:(blk + 1) * 128], identb[:, :],
                   [rv, R_identb], [R_bank[6]], inc=(blk == 3))
            CP("ACT" if j % 2 == 0 else "DVE", vts[:, newslots[j], :], b6, [R_bank[6]], [R_vt[newslots[j]]])

    def attn_phase(first_pass):
        acc, racc = sm("ssqa", 24, 4)
        S.op("DVE", lambda e: e.memset(acc, 0.0), [], [racc])
        pending_q = None
        for hg in range(4):
            newslots = [free.pop(0) for _ in range(4)]
            if pending_q is None:
                proj_group(w_in, hg * 512, 8, xn_rhs, R_xn)
            else:
                for _ in pending_q:
                    pass
            for j in range(4):
                ACT(qT[:, j, :], banks[j][:, :], AF.Copy, [R_bank[j]], [R_qT[j]], scale=float(128 ** -0.5))
            kv_group(hg, newslots)
            proj_group(w_in, 6144 + hg * 512, 8, xn_rhs, R_xn)
            for j in range(4):
                ACT(f4[:, j, 0:512], banks[j][:, :], AF.Silu, [R_bank[j]], [R_f4[j]])
            state = {}
            fillgen = proj_group_steps(w_in, (hg + 1) * 512, 8, xn_rhs, R_xn) if (hg < 3 and OPT_FILL) else None

            def fill(n=1):
                if fillgen is not None:
                    for _ in range(n):
                        next(fillgen, None)

            def stageA(j, pi):
                h = hg * 4 + j
                hs, ns = hist[h], newslots[j]
                if pi == 0:
                    bias, rb = LP.get()
                    DMA("SP", bias[:, :], bias_d[h, :, :], f"bias{LP.i % 2}", [], [rb])
                    state[("bias", j)] = (bias, rb)
                bias, rb = state[("bias", j)]
                nh = 512 - 128 * pi
                ncur = 128 * (pi + 1)
                lq = qT[:, j, pi * 128:(pi + 1) * 128]
                MM(banks[4][:, 0:nh], lq, kts[:, hs, 128 * pi:512], True, True, [R_qT[j], R_kt[hs]], [R_bank[4]])
                MM(banks[5][:, 0:ncur], lq, kts[:, ns, 0:ncur], True, True, [R_qT[j], R_kt[ns]], [R_bank[5]])
                ssb, rss = FP.get()
                if first_pass:
                    STT(ssb[:, 0:nh], banks[4][:, 0:nh], cvc(C_HM), bias[:, 0:nh], ALU.add, ALU.add,
                        [R_bank[4], rb, R_cvec], [rss])
                else:
                    TTo(ssb[:, 0:nh], banks[4][:, 0:nh], bias[:, 0:nh], ALU.add, [R_bank[4], rb], [rss])
                TTo(ssb[:, nh:640], banks[5][:, 0:ncur], bias[:, nh:640], ALU.add, [R_bank[5], rb], [rss])
                u = (j * 4 + pi) % 2
                mx, rmx = sm(f"mx{u}", 28 + u, 1)
                S.op("DVE", lambda e, ssb=ssb, mx=mx: e.tensor_reduce(out=mx, in_=ssb[:, :], axis=AX.X, op=ALU.max,
                                                                     negate=True), [rss], [rmx])
                ex, rex = FP.get()
                rsum, rrs = sm(f"rsum{u}", 30 + u, 1)
                ACT(ex[:, :], ssb[:, :], AF.Exp, [rss, rmx], [rex, rrs], bias=mx, accum=rsum)
                state[("ex", j, pi)] = (ex, rex, rsum, rrs)

            def stageA2(j, pi):
                ex, rex, rsum, rrs = state.pop(("ex", j, pi))
                S.op("DVE", lambda e, rsum=rsum: e.reciprocal(out=rsum, in_=rsum), [rrs], [rrs])
                pn, rpn = BP.get()
                TS(pn[:, :], ex[:, :], rsum, ALU.mult, [rex, rrs], [rpn])
                state[("pn", j, pi)] = (pn, rpn)

            def stageB(j, pi):
                h = hg * 4 + j
                hs, ns = hist[h], newslots[j]
                pn, rpn = state.pop(("pn", j, pi))
                b6 = banks[6][:, 0:320].bitcast(BF16)
                for b in range(5):
                    TR(b6[:, b * 128:(b + 1) * 128], pn[:, b * 128:(b + 1) * 128], identb[:, :],
                       [rpn, R_identb], [R_bank[6]], inc=(b == 4))
                pt, rpt = BP.get()
                CP("ACT", pt[:, :], b6, [R_bank[6]], [rpt])
                fill(1)
                for b in range(5):
                    kb = pi + b
                    if kb < 4:
                        vblk, rvv = vts[:, hs, kb * 128:(kb + 1) * 128], R_vt[hs]
                    else:
                        vblk, rvv = vts[:, ns, (kb - 4) * 128:(kb - 3) * 128], R_vt[ns]
                    MM(banks[7][:, pi * 128:(pi + 1) * 128], vblk, pt[:, b * 128:(b + 1) * 128],
                       start=(b == 0), stop=(b == 4), reads=[rvv, rpt], writes=[R_bank[7]], inc=(b == 4))
                if pi == 3:
                    ssq_cols(banks[7][:, :], [R_bank[7]], acc, racc)
                    STT(yT[:, h, :], banks[7][:, :], cvc(C_AOG + h), f4[:, j, 0:512], ALU.mult, ALU.mult,
                        [R_bank[7], R_cvec, R_f4[j]], [R_yT[h]])

            units = [(j, pi) for j in range(4) for pi in range(4)]
            if OPT_PIPE:
                for ui, (j, pi) in enumerate(units):
                    stageA(j, pi)
                    fill(1)
                    if ui > 0:
                        stageA2(*units[ui - 1])
                        stageB(*units[ui - 1])
                stageA2(*units[-1])
                stageB(*units[-1])
            else:
                for (j, pi) in units:
                    stageA(j, pi)
                    stageA2(j, pi)
                    stageB(j, pi)
            for j in range(4):
                h = hg * 4 + j
                free.append(hist[h])
                hist[h] = newslots[j]
            pending_q = fillgen
        ra, rra = sm("rstda", 32, 4)
        rstd_from(acc, ra, 4, [racc], [rra], 1.0 / AW)
        return ra, rra

    def halo_kv():
        for hg in range(4):
            newslots = [hist[hg * 4 + j] for j in range(4)]
            kv_group(hg, newslots)

    xrc = {"i": 0}
    outc = {"i": 0}

    def out_phase(t0x, t0y, ra, rra, rc, rrc):
        for cb in range(8):
            for tt in range(4):
                si = xrc["i"] % 8
                xrc["i"] += 1
                DMA("SP", hview[:, tt, cb * 512:(cb + 1) * 512],
                    xc[t0x + tt * 128:t0x + (tt + 1) * 128, cb * 512:(cb + 1) * 512], f"xr{si}", [], R_h(tt, cb))
            for g in range(8):
                slot, rs = wload(w_out[g * 512:(g + 1) * 512, cb * 512:(cb + 1) * 512])
                for kc in range(NKS):
                    k = g * NKS + kc
                    for tt in range(4):
                        bk = (0 if g < 4 else 4) + tt
                        last = k in (15, 31)
                        MM(banks[bk][:, :], yT[:, k, tt * 128:(tt + 1) * 128], slot[:, kc, :], start=(k in (0, 16)),
                           stop=last, reads=[rs, R_yT[k]], writes=[R_bank[bk]], inc=(last or (kc == NKS - 1 and tt == 3)))
                if g == 3:
                    for tt in range(4):
                        hv = hview[:, tt, cb * 512:(cb + 1) * 512]
                        STT(hv, banks[tt][:, :], ra[:, tt:tt + 1], hv, ALU.mult, ALU.add,
                            [R_bank[tt], rra] + R_h(tt, cb), R_h(tt, cb))
                if g == 7:
                    for tt in range(4):
                        hv = hview[:, tt, cb * 512:(cb + 1) * 512]
                        STT(hv, banks[4 + tt][:, :], rc[:, tt:tt + 1], hv, ALU.mult, ALU.add,
                            [R_bank[4 + tt], rrc] + R_h(tt, cb), R_h(tt, cb))
                        ssq_block(0, tt, cb, hv, R_h(tt, cb))
        rh, rrh = sm("rstdh", 36, 4)
        for tt in range(4):
            rstd_rows_from_blocks(0, tt, (rh[:, tt:tt + 1], rrh), 1.0 / D)
        for tt in range(4):
            for k4 in range(8):
                bk = 6 + (k4 % 2)
                for j in range(4):
                    kc = k4 * 4 + j
                    TR(banks[bk][:, j * 128:(j + 1) * 128], hview[:, tt, kc * 128:(kc + 1) * 128], ident[:, :],
                       R_h(tt, kc // 4) + [R_ident], [R_bank[bk]], inc=(j == 3))
                for j in range(4):
                    kc = k4 * 4 + j
                    TS(yT[:, kc, tt * 128:(tt + 1) * 128], banks[bk][:, j * 128:(j + 1) * 128], cvc(C_PNG + kc),
                       ALU.mult, [R_bank[bk], R_cvec], [R_yT[kc]])
        for tt in range(4):
            ptile, rp = FP.get()
            DMA("SP", ptile[:, 0:256], pc[t0y + tt * 128:t0y + (tt + 1) * 128, :], f"p{tt % 2}", [], [rp])
            for k2 in range(2):
                TR(banks[5][:, k2 * 128:(k2 + 1) * 128], ptile[:, k2 * 128:(k2 + 1) * 128], ident[:, :],
                   [rp, R_ident], [R_bank[5]], inc=(k2 == 1))
            for k2 in range(2):
                CP("ACT", qT[:, k2, tt * 128:(tt + 1) * 128], banks[5][:, k2 * 128:(k2 + 1) * 128], [R_bank[5]], [R_qT[k2]])
        for cb in range(8):
            bbc, rbb = LP.get()
            DMA("SP", bbc[:, 0:512], bpg_d[0:1, cb * 512:(cb + 1) * 512].partition_broadcast(128), f"bb{cb % 2}", [], [rbb])
            for g in range(8):
                slot, rs = wload(w_pg[g * 512:(g + 1) * 512, cb * 512:(cb + 1) * 512])
                for kc in range(NKS):
                    k = g * NKS + kc
                    for tt in range(4):
                        last = (k == 31)
                        MM(banks[tt][:, :], yT[:, k, tt * 128:(tt + 1) * 128], slot[:, kc, :], start=(k == 0), stop=last,
                           reads=[rs, R_yT[k]], writes=[R_bank[tt]], inc=(last or (kc == NKS - 1 and tt == 3)))
            pslot, rps = wload(w_ple[:, cb * 512:(cb + 1) * 512], nk=2)
            for tt in range(4):
                for k2 in range(2):
                    MM(banks[4 + tt][:, :], qT[:, k2, tt * 128:(tt + 1) * 128], pslot[:, k2, :], start=(k2 == 0),
                       stop=(k2 == 1), reads=[rps, R_qT[k2]], writes=[R_bank[4 + tt]], inc=(k2 == 1))
            ts_ = []
            for tt in range(4):
                t, rt = FP.get()
                ts_.append((t, rt))
                STT(t[:, 0:512], banks[tt][:, :], rh[:, tt:tt + 1], bbc[:, 0:512], ALU.mult, ALU.add,
                    [R_bank[tt], rrh, rbb], [rt])
            for tt in range(4):
                t, rt = ts_[tt]
                ACT(t[:, 0:512], t[:, 0:512], AF.Sigmoid, [rt], [rt])
            for tt in range(4):
                t, rt = ts_[tt]
                TTo(t[:, 0:512], t[:, 0:512], banks[4 + tt][:, :], ALU.mult, [rt, R_bank[4 + tt]], [rt])
                hv = hview[:, tt, cb * 512:(cb + 1) * 512]
                TTo(hv, hv, t[:, 0:512], ALU.add, R_h(tt, cb) + [rt], R_h(tt, cb))
                ssq_block(32, tt, cb, hv, R_h(tt, cb))
        rf, rrf = sm("rstdf", 40, 4)
        for tt in range(4):
            rstd_rows_from_blocks(32, tt, (rf[:, tt:tt + 1], rrf), 1.0 / D)
        for cb in range(8):
            fgb, rfg = LP.get()
            DMA("SP", fgb[:, 0:512], fg_d[0:1, cb * 512:(cb + 1) * 512].partition_broadcast(128), f"fg{cb % 2}", [], [rfg])
            for tt in range(4):
                o, ro = FP.get()
                STT(o[:, 0:512], hview[:, tt, cb * 512:(cb + 1) * 512], rf[:, tt:tt + 1], fgb[:, 0:512], ALU.mult, ALU.mult,
                    R_h(tt, cb) + [rrf, rfg], [ro])
                si = outc["i"] % 4
                outc["i"] += 1
                DMA("SP", y[t0y + tt * 128:t0y + (tt + 1) * 128, cb * 512:(cb + 1) * 512], o[:, 0:512], f"o{si}", [ro], [])

    if HALO:
        phase0(0)
        for k in range(32):
            CP("ACT" if k % 2 else "DVE", xh[:, k, :], xnT[:, k, 480:512], R_xn(k), [R_xh])
        halo_kv()
    else:
        S.op("DVE", lambda e: e.memset(xh[:, :, :], 0.0), [], [R_xh])
    for pp in range(NPASS):
        t0y = pp * TT
        t0x = t0y + (TT if HALO else 0)
        phase0(t0x)
        conv_phase(pp == 0 and OPT_XH)
        rc, rrc = pointwise_phase()
        ra, rra = attn_phase(pp == 0)
        out_phase(t0x, t0y, ra, rra, rc, rrc)
    S.final_wait_all("SP")

    sem_names = set(ENG) | set(S.dcnt.keys())
    sems = {}
    import contextlib
    with contextlib.ExitStack() as st:
        for n in sorted(sem_names):
            sems[n] = st.enter_context(nc.semaphore("s_" + n))
        block = st.enter_context(nc.Block())

        def emit(eng_obj, name):
            for item in S.prog[name]:
                if item[0] == "wait":
                    eng_obj.wait_ge(sems[item[1]], item[2])
                else:
                    ins = item[1](eng_obj)
                    if item[3]:
                        ins.then_inc(sems[item[2]], item[3])

        @block.tensor
        def _(e):
            emit(e, "PE")

        @block.scalar
        def _(e):
            emit(e, "ACT")

        @block.vector
        def _(e):
            emit(e, "DVE")

        @block.gpsimd
        def _(e):
            emit(e, "POOL")

        @block.sync
        def _(e):
            emit(e, "SP")
    return nc, S


def host_prep(inp):
    f = np.float32
    g = lambda k: np.asarray(inp[k], dtype=f)
    col = lambda v, n: np.ascontiguousarray(v.reshape(n, 128).T)
    cvec = np.zeros((128, NCV), f)
    cvec[:, C_NING:C_NING + 32] = col(g("norm_in_g")[0], 32)
    cvec[:, C_PNG:C_PNG + 32] = col(g("ple_norm_g")[0], 32)
    cvec[:, C_BDW:C_BDW + 16] = col(g("b_dw")[0], 16)
    cvec[:, C_LNG:C_LNG + 16] = col(g("conv_ln_g")[0], 16)
    cvec[:, C_LNB:C_LNB + 16] = col(g("conv_ln_b")[0], 16)
    cvec[:, C_BPW:C_BPW + 16] = col(g("b_pw")[0], 16)
    cvec[:, C_AOG:C_AOG + 16] = col(g("attn_out_g")[0], 16)
    cvec[:, C_COG:C_COG + 16] = col(g("conv_out_g")[0], 16)
    wdw = g("w_dw")[0].reshape(31, 16, 128).transpose(2, 1, 0)
    cvec[:, C_WDW:C_WDW + 496] = wdw.reshape(128, 496)
    q = np.arange(128)[:, None]
    j = np.arange(640)[None, :]
    idx = np.clip(512 + q - j, -256, 256) + 256
    tab = g("rel_table")[0]
    bias = tab[:, idx]
    qc, kc = q // 64, j // 64
    valid = (kc >= qc) & (kc <= 8 + qc)
    bias = np.where(valid[None], bias, f(NEG)).astype(f)
    wd = g("w_dw")[0]
    wdg = np.zeros((16, 128, 32, 128), f)
    ar = np.arange(128)
    for cc in range(16):
        wdg[cc, ar, :31, ar] = wd[:, cc * 128:(cc + 1) * 128].T
    shared = {
        "wdg": wdg.reshape(16 * 128, 4096),
        "w_in": np.ascontiguousarray(g("w_in")[0]), "w_pw": np.ascontiguousarray(g("w_pw")[0]),
        "w_out": np.ascontiguousarray(g("w_out")[0]), "w_pg": np.ascontiguousarray(g("w_ple_gate")[0]),
        "w_ple": np.ascontiguousarray(g("w_ple")[0]), "cvec": cvec,
        "bpg": np.ascontiguousarray(g("b_ple_gate")[0][None, :]), "fg": np.ascontiguousarray(g("final_g")[None, :]),
        "bias": np.ascontiguousarray(bias), "ident": np.eye(128, dtype=f),
    }
    return shared


_CACHE = {}


def kernel(**inp):
    x = np.asarray(inp["x"], dtype=np.float32)
    p = np.asarray(inp["p"], dtype=np.float32)[0]
    B, T, _ = x.shape
    shared = host_prep(inp)
    if "nc" not in _CACHE:
        _CACHE["nc"] = build(4, True)[0]
    nc = _CACHE["nc"]
    in_maps = []
    for c in range(8):
        b, half = c // 2, c % 2
        s0 = half * 2048
        xcore = np.zeros((2560, D), np.float32)
        if half == 1:
            xcore[0:512] = x[b, s0 - 512:s0]
        xcore[512:] = x[b, s0:s0 + 2048]
        m = dict(shared)
        m["xc"] = xcore
        m["pc"] = np.ascontiguousarray(p[b, s0:s0 + 2048])
        cv2 = shared["cvec"].copy()
        cv2[:, C_HM] = 0.0 if half == 1 else NEG
        m["cvec"] = cv2
        in_maps.append(m)
    res = run_bass_kernel_spmd(nc, in_maps, core_ids=list(range(8)))
    out = np.zeros((B, T, D), np.float32)
    for c in range(8):
        b, half = c // 2, c % 2
        out[b, half * 2048:(half + 1) * 2048] = res.results[c]["y"]
    return out
```
